# Optimizing a Trainium2 kernel written in Bass

```python
import math
import jax
import jax.numpy as jnp
from jax import lax
import numpy as np

D_MODEL = 1024
BATCH = 2
SEQ = 8192
DEPTH = 2

GRID_W = 64
CTX_LEN = 256

NA_HEADS = 8
HEAD_DIM = 64
NA_W = NA_HEADS * HEAD_DIM
NA_WIN_R = 8
NA_WIN_C = 16
ROPE_THETA = 10000.0
HY_W = 256
HY_ORDER = 2
HY_SHORT = 3
HY_BANDS = 16
HY_EMB = 1 + 2 * HY_BANDS
HY_FO = 64
HY_DECAY_TARGET = 1e-2
HY_FAST_DECAY = 0.3
HY_SLOW_DECAY = 1.5
POOL_W = 256
POOL_SIZES = (2, 4, 8, 16)
POOL_GROUP = POOL_W // len(POOL_SIZES)

MIX_W = NA_W + HY_W + POOL_W
IN_W = 3 * NA_W + (HY_ORDER + 1) * HY_W + POOL_W

N_EXPERTS = 16
EC_FACTOR = 2
D_EXPERT = 2048

EPS = 1e-6

kernel_name = 'hybrid_natten_hyena_pool_ecmoe_dit'


def rmsnorm(x, w):
    xf = x.astype(jnp.float32)
    y = xf * lax.rsqrt(jnp.mean(xf * xf, axis=-1, keepdims=True) + EPS)
    return (y * w.astype(jnp.float32)).astype(x.dtype)


def modulate(h, shift, scale):
    return h * (1.0 + scale) + shift


def split_heads(u):
    return u.reshape(u.shape[:-1] + (NA_HEADS, HEAD_DIM))


def axial_rope(n_tokens):
    t = jnp.arange(n_tokens, dtype=jnp.int32)
    pos = jnp.stack([t // GRID_W, t % GRID_W], axis=-1).astype(jnp.float32)
    nf = HEAD_DIM // 4
    inv = ROPE_THETA ** (-jnp.arange(nf, dtype=jnp.float32) / nf)
    ang = pos[:, :, None] * inv
    return jnp.cos(ang), jnp.sin(ang)


def apply_rope(x, cos, sin):
    B, N, H, dh = x.shape
    xr = x.reshape(B, N, H, 2, 2, dh // 4)
    a, b = xr[..., 0, :], xr[..., 1, :]
    c = cos[None, :, None].astype(x.dtype)
    s = sin[None, :, None].astype(x.dtype)
    return jnp.stack([a * c - b * s, b * c + a * s], axis=-2).reshape(B, N, H, dh)


def dense_attention(q, k, v):
    s = jnp.einsum('bqhd,bkhd->bhqk', q, k).astype(jnp.float32) * (q.shape[-1] ** -0.5)
    p = jax.nn.softmax(s, axis=-1).astype(v.dtype)
    return jnp.einsum('bhqk,bkhd->bqhd', p, v)


def neighborhood_attention(q, k, v, q_plain, kc, vc, rpb):
    B, N, H, dh = q.shape
    rows = N // GRID_W
    wr = min(NA_WIN_R, rows)
    scale = dh ** -0.5
    grid = lambda a: a.reshape(B, rows, GRID_W, H, dh)
    qg, qpg, kg, vg = grid(q), grid(q_plain), grid(k), grid(v)
    col = jnp.arange(GRID_W)
    col_start = jnp.clip(col - NA_WIN_C // 2, 0, GRID_W - NA_WIN_C)
    col_idx = col_start[:, None] + jnp.arange(NA_WIN_C)[None, :]
    col_bias_idx = col_idx - col[:, None] + (NA_WIN_C - 1)
    rpb_f = rpb.astype(jnp.float32)

    def row_block(r):
        rs = jnp.clip(r - NA_WIN_R // 2, 0, rows - wr)
        q_r = lax.dynamic_index_in_dim(qg, r, axis=1, keepdims=False)
        qp_r = lax.dynamic_index_in_dim(qpg, r, axis=1, keepdims=False)
        k_win = lax.dynamic_slice_in_dim(kg, rs, wr, axis=1)[:, :, col_idx]
        v_win = lax.dynamic_slice_in_dim(vg, rs, wr, axis=1)[:, :, col_idx]
        s_lat = jnp.einsum('bqhd,biqjhd->bhqij', q_r, k_win).astype(jnp.float32) * scale
        row_bias_idx = rs + jnp.arange(wr) - r + (NA_WIN_R - 1)
        bias = rpb_f[:, row_bias_idx[:, None, None], col_bias_idx[None]].transpose(0, 2, 1, 3)
        s_lat = (s_lat + bias).reshape(B, H, GRID_W, wr * NA_WIN_C)
        s_ctx = jnp.einsum('bqhd,bkhd->bhqk', qp_r, kc).astype(jnp.float32) * scale
        p = jax.nn.softmax(jnp.concatenate([s_lat, s_ctx], axis=-1), axis=-1).astype(v.dtype)
        p_lat = p[..., :wr * NA_WIN_C].reshape(B, H, GRID_W, wr, NA_WIN_C)
        p_ctx = p[..., wr * NA_WIN_C:]
        return (jnp.einsum('bhqij,biqjhd->bqhd', p_lat, v_win)
                + jnp.einsum('bhqk,bkhd->bqhd', p_ctx, vc))

    out = lax.map(row_block, jnp.arange(rows))
    return out.transpose(1, 0, 2, 3, 4).reshape(B, N, H * dh)


def short_conv(u, w, b):
    L = u.shape[1]
    pad = HY_SHORT // 2
    up = jnp.pad(u, ((0, 0), (pad, pad), (0, 0)))
    y = b
    for j in range(HY_SHORT):
        y = y + up[:, j:j + L] * w[j]
    return y


def hyena_filter(L, w1, b1, w2, b2, w3, freq):
    f32 = jnp.float32
    t = jnp.linspace(0.0, 1.0, L, dtype=f32)[:, None]
    w = (2.0 * math.pi / L) * jnp.arange(L, dtype=f32)[:, None]
    f = jnp.linspace(1e-4, HY_BANDS - 1, HY_BANDS, dtype=f32)[None, :]
    z = jnp.concatenate([t, jnp.cos(f * w), -jnp.sin(f * w)], axis=-1)
    fr = freq.astype(f32)
    h = jnp.sin(fr * (z @ w1.astype(f32) + b1.astype(f32)))
    h = jnp.sin(fr * (h @ w2.astype(f32) + b2.astype(f32)))
    h = (h @ w3.astype(f32)).reshape(L, HY_ORDER, 2, HY_W)
    deltas = jnp.abs(jnp.linspace(math.log(HY_DECAY_TARGET) / HY_SLOW_DECAY,
                                  math.log(HY_DECAY_TARGET) / HY_FAST_DECAY, HY_W, dtype=f32))
    h = h * jnp.exp(-t * deltas)[:, None, None, :]
    hf, hb = h[:, :, 0], h[:, :, 1]
    k2 = jnp.concatenate([hf, jnp.zeros_like(hf[:1]), hb[:0:-1]], axis=0)
    k2 = k2 * lax.rsqrt(jnp.sum(k2 * k2, axis=0, keepdims=True) + EPS)
    return jnp.fft.rfft(k2, axis=0)


def fft_conv(u, kf):
    L = u.shape[1]
    U = jnp.fft.rfft(u.astype(jnp.float32), n=2 * L, axis=1)
    return jnp.fft.irfft(U * kf[None], n=2 * L, axis=1)[:, :L].astype(u.dtype)


def hyena_mixer(u, kf, conv_w, conv_b, skip):
    u = short_conv(u, conv_w, conv_b)
    x1, x2, v = jnp.split(u, 3, axis=-1)
    z = v
    for o, gate in enumerate((x1, x2)):
        z = gate * (fft_conv(z, kf[:, o]) + z * skip[o])
    return z


def pool_mixer(u, pool_w, pool_scale):
    B, L, _ = u.shape
    uf = u.astype(jnp.float32)
    csum = jnp.pad(jnp.cumsum(uf, axis=1), ((0, 0), (1, 0), (0, 0)))
    t = jnp.arange(L)
    outs = []
    for g, w in enumerate(POOL_SIZES):
        lo = jnp.clip(t - w // 2, 0, L)
        hi = jnp.clip(t - w // 2 + w, 0, L)
        sl = slice(g * POOL_GROUP, (g + 1) * POOL_GROUP)
        mean = (csum[:, hi, sl] - csum[:, lo, sl]) / (hi - lo).astype(jnp.float32)[:, None]
        outs.append(jnp.einsum('blc,cd->bld', (mean - uf[..., sl]).astype(u.dtype), pool_w[g]))
    return jnp.concatenate(outs, axis=-1) * pool_scale


def expert_choice_moe(h, w_router, w_gate, w_up, w_down):
    B, N, _ = h.shape
    cap = EC_FACTOR * N // N_EXPERTS
    aff = jax.nn.softmax(jnp.einsum('bnd,de->bne', h, w_router).astype(jnp.float32), axis=-1)
    gate, idx = lax.top_k(aff.transpose(0, 2, 1), cap)
    bidx = jnp.arange(B)[:, None, None]
    xg = h[bidx, idx]
    a = jnp.einsum('becd,edf->becf', xg, w_gate)
    b = jnp.einsum('becd,edf->becf', xg, w_up)
    y = jnp.einsum('becf,efd->becd', jax.nn.silu(a) * b, w_down)
    y = (y * gate[..., None].astype(y.dtype)).astype(h.dtype)
    return jnp.zeros_like(h).at[bidx, idx].add(y)


def trunk_layer(xl, xc, c, c_ctx, rope_cos, rope_sin, p, last):
    B, N, _ = xl.shape
    mod_l = (jax.nn.silu(c) @ p['w_mod'] + p['b_mod'])[:, None, :]
    mod_c = jax.nn.silu(c_ctx) @ p['w_mod'] + p['b_mod']
    sh1, sc1, g1, sh2, sc2, g2 = jnp.split(mod_l, 6, axis=-1)
    csh1, csc1, cg1, csh2, csc2, cg2 = jnp.split(mod_c, 6, axis=-1)
    w_in = p['w_in']
    hy_cols = slice(3 * NA_W, 3 * NA_W + 3 * HY_W)
    pool_cols = slice(3 * NA_W + 3 * HY_W, IN_W)

    hc = modulate(rmsnorm(xc, p['norm1_w']), csh1, csc1)
    if last:
        uc = hc @ w_in[:, NA_W:3 * NA_W]
        kc = rmsnorm(split_heads(uc[..., :NA_W]), p['k_norm_w'])
        vc = split_heads(uc[..., NA_W:])
        xc_new = None
    else:
        uc = hc @ w_in
        qc = rmsnorm(split_heads(uc[..., :NA_W]), p['q_norm_w'])
        kc = rmsnorm(split_heads(uc[..., NA_W:2 * NA_W]), p['k_norm_w'])
        vc = split_heads(uc[..., 2 * NA_W:3 * NA_W])
        ctx_kf = hyena_filter(xc.shape[1], p['hy_w1'], p['hy_b1'], p['hy_w2'], p['hy_b2'], p['hy_w3'], p['hy_freq'])
        mix_c = jnp.concatenate([
            dense_attention(qc, kc, vc).reshape(B, xc.shape[1], NA_W),
            hyena_mixer(uc[..., hy_cols], ctx_kf, p['hy_conv_w'], p['hy_conv_b'], p['hy_skip']),
            pool_mixer(uc[..., pool_cols], p['pool_w'], p['pool_scale'])], axis=-1)
        xc_new = xc + cg1 * (mix_c @ p['w_out'])
        h2c = modulate(rmsnorm(xc_new, p['norm2_w']), csh2, csc2)
        xc_new = xc_new + cg2 * expert_choice_moe(h2c, p['w_router'], p['w_gate'], p['w_up'], p['w_down'])

    hl = modulate(rmsnorm(xl, p['norm1_w']), sh1, sc1)
    ul = hl @ w_in
    ql = rmsnorm(split_heads(ul[..., :NA_W]), p['q_norm_w'])
    kl = apply_rope(rmsnorm(split_heads(ul[..., NA_W:2 * NA_W]), p['k_norm_w']), rope_cos, rope_sin)
    vl = split_heads(ul[..., 2 * NA_W:3 * NA_W])
    o_na = neighborhood_attention(apply_rope(ql, rope_cos, rope_sin), kl, vl, ql, kc, vc, p['na_rpb'])
    lat_kf = hyena_filter(N, p['hy_w1'], p['hy_b1'], p['hy_w2'], p['hy_b2'], p['hy_w3'], p['hy_freq'])
    mix_l = jnp.concatenate([
        o_na,
        hyena_mixer(ul[..., hy_cols], lat_kf, p['hy_conv_w'], p['hy_conv_b'], p['hy_skip']),
        pool_mixer(ul[..., pool_cols], p['pool_w'], p['pool_scale'])], axis=-1)
    xl_new = xl + g1 * (mix_l @ p['w_out'])
    h2l = modulate(rmsnorm(xl_new, p['norm2_w']), sh2, sc2)
    xl_new = xl_new + g2 * expert_choice_moe(h2l, p['w_router'], p['w_gate'], p['w_up'], p['w_down'])
    return xl_new, xc_new


def setup_inputs(seed: int = 0) -> dict:
    key = jax.random.key(seed)
    ks = iter(jax.random.split(key, 40))
    D = D_MODEL

    def nrm(shape, scale):
        return jax.random.normal(next(ks), shape, jnp.float32) * scale

    return {
        'x': nrm((BATCH, SEQ, D), 1.0),
        'c': nrm((BATCH, D), 1.0),
        'ctx': nrm((BATCH, CTX_LEN, D), 1.0),
        'c_ctx': nrm((D,), 1.0),
        'w_mod': nrm((DEPTH, D, 6 * D), 0.5 * D ** -0.5),
        'b_mod': nrm((DEPTH, 6 * D), 0.02),
        'norm1_w': 1.0 + nrm((DEPTH, D), 0.02),
        'norm2_w': 1.0 + nrm((DEPTH, D), 0.02),
        'w_in': nrm((DEPTH, D, IN_W), D ** -0.5),
        'w_out': nrm((DEPTH, MIX_W, D), MIX_W ** -0.5),
        'q_norm_w': 1.0 + nrm((DEPTH, HEAD_DIM), 0.02),
        'k_norm_w': 1.0 + nrm((DEPTH, HEAD_DIM), 0.02),
        'na_rpb': nrm((DEPTH, NA_HEADS, 2 * NA_WIN_R - 1, 2 * NA_WIN_C - 1), 0.1),
        'hy_conv_w': nrm((DEPTH, HY_SHORT, 3 * HY_W), HY_SHORT ** -0.5),
        'hy_conv_b': nrm((DEPTH, 3 * HY_W), 0.02),
        'hy_w1': nrm((DEPTH, HY_EMB, HY_FO), HY_EMB ** -0.5),
        'hy_b1': nrm((DEPTH, HY_FO), 0.02),
        'hy_w2': nrm((DEPTH, HY_FO, HY_FO), HY_FO ** -0.5),
        'hy_b2': nrm((DEPTH, HY_FO), 0.02),
        'hy_w3': nrm((DEPTH, HY_FO, HY_ORDER * 2 * HY_W), HY_FO ** -0.5),
        'hy_freq': 1.0 + nrm((DEPTH, HY_FO), 0.1),
        'hy_skip': nrm((DEPTH, HY_ORDER, HY_W), 0.5),
        'pool_w': nrm((DEPTH, len(POOL_SIZES), POOL_GROUP, POOL_GROUP), POOL_GROUP ** -0.5),
        'pool_scale': 1.0 + nrm((DEPTH, POOL_W), 0.02),
        'w_router': nrm((DEPTH, D, N_EXPERTS), D ** -0.5),
        'w_gate': nrm((DEPTH, N_EXPERTS, D, D_EXPERT), D ** -0.5),
        'w_up': nrm((DEPTH, N_EXPERTS, D, D_EXPERT), D ** -0.5),
        'w_down': nrm((DEPTH, N_EXPERTS, D_EXPERT, D), D_EXPERT ** -0.5),
    }


def reference(x, c, ctx, c_ctx, w_mod, b_mod, norm1_w, norm2_w, w_in, w_out, q_norm_w, k_norm_w, na_rpb,
              hy_conv_w, hy_conv_b, hy_w1, hy_b1, hy_w2, hy_b2, hy_w3, hy_freq, hy_skip,
              pool_w, pool_scale, w_router, w_gate, w_up, w_down):
    rope_cos, rope_sin = axial_rope(x.shape[1])
    xl, xc = x, ctx
    for i in range(DEPTH):
        p = {
            'w_mod': w_mod[i], 'b_mod': b_mod[i], 'norm1_w': norm1_w[i], 'norm2_w': norm2_w[i],
            'w_in': w_in[i], 'w_out': w_out[i], 'q_norm_w': q_norm_w[i], 'k_norm_w': k_norm_w[i],
            'na_rpb': na_rpb[i], 'hy_conv_w': hy_conv_w[i], 'hy_conv_b': hy_conv_b[i],
            'hy_w1': hy_w1[i], 'hy_b1': hy_b1[i], 'hy_w2': hy_w2[i], 'hy_b2': hy_b2[i], 'hy_w3': hy_w3[i],
            'hy_freq': hy_freq[i], 'hy_skip': hy_skip[i], 'pool_w': pool_w[i], 'pool_scale': pool_scale[i],
            'w_router': w_router[i], 'w_gate': w_gate[i], 'w_up': w_up[i], 'w_down': w_down[i],
        }
        xl, xc = trunk_layer(xl, xc, c, c_ctx, rope_cos, rope_sin, p, i == DEPTH - 1)
    return xl
```

```python
import math
import numpy as np
import ml_dtypes
import concourse.bass as bass
import concourse.mybir as mybir
from concourse.bass_utils import run_bass_kernel_spmd

F32 = mybir.dt.float32
BF16 = mybir.dt.bfloat16
I32 = mybir.dt.int32
ALU = mybir.AluOpType
AF = mybir.ActivationFunctionType
AX = mybir.AxisListType
NPBF = ml_dtypes.bfloat16

D = 1024
B = 2
N = 8192
GW = 64
CTX = 256
NAW = 512
HYW = 256
POOLW = 256
INW = 2560
NE = 16
DE = 2048
EPS = 1e-6
NCORES = 8
TPC = 17
ROWS = TPC * 128


class Prog:
    def __init__(self, nc, same_engine_sync=True):
        self.nc = nc
        self.eng = {'pe': nc.tensor, 'dve': nc.vector, 'act': nc.scalar, 'pool': nc.gpsimd, 'sp': nc.sync}
        self.sem = {}
        self.cnt = {}
        self._cms = []
        self.semobj = {}
        for e in self.eng:
            cm = nc.semaphore('prog_' + e)
            self.sem[e] = cm.__enter__()
            self._cms.append(cm)
            self.cnt[e] = 0
            self.semobj['prog_' + e] = self.sem[e]
        self.seen = {e: {} for e in self.eng}
        self.lastw = {}
        self.readers = {}
        self.dsem = {}
        self.same = same_engine_sync
        self.tensors = []

    def sb(self, name, shape, dt):
        cm = self.nc.sbuf_tensor(name, shape, dt)
        t = cm.__enter__()
        self._cms.append(cm)
        return t

    def ps(self, name, shape, dt):
        cm = self.nc.psum_tensor(name, shape, dt)
        t = cm.__enter__()
        self._cms.append(cm)
        return t

    def close(self):
        for cm in reversed(self._cms):
            cm.__exit__(None, None, None)

    @staticmethod
    def _hit(table, key):
        name, sub = key
        out = []
        for k2, v in table.items():
            if k2[0] != name:
                continue
            if sub is None or k2[1] is None or k2[1] == sub:
                out.append((k2, v))
        return out

    def _wait(self, e, tok):
        semname, val, src = tok
        if src == e and (e == 'pe' or not self.same):
            return
        if self.seen[e].get(semname, 0) >= val:
            return
        self.seen[e][semname] = val
        self.eng[e].wait_ge(self.semobj[semname], val)

    def _deps(self, e, reads, writes):
        for k in reads:
            for _, tok in self._hit(self.lastw, k):
                self._wait(e, tok)
        for k in writes:
            for _, tok in self._hit(self.lastw, k):
                self._wait(e, tok)
            for _, toks in self._hit(self.readers, k):
                for tok in toks:
                    self._wait(e, tok)

    def _commit(self, tok, reads, writes):
        for k in writes:
            for kk, _ in self._hit(self.lastw, k):
                if kk != k and k[1] is None:
                    del self.lastw[kk]
            for kk, _ in self._hit(self.readers, k):
                if k[1] is None or kk == k:
                    del self.readers[kk]
            self.lastw[k] = tok
        for k in reads:
            lst = self.readers.setdefault(k, [])
            lst.append(tok)
            if len(lst) > 8:
                best = {}
                for t in lst:
                    if t[0] not in best or best[t[0]][1] < t[1]:
                        best[t[0]] = t
                self.readers[k] = list(best.values())

    def op(self, e, fn, reads=(), writes=()):
        self._deps(e, reads, writes)
        ins = fn()
        self.cnt[e] += 1
        ins.then_inc(self.sem[e], 1)
        tok = ('prog_' + e, self.cnt[e], e)
        self._commit(tok, reads, writes)
        return tok

    def _dsem(self, sk):
        sk = sk[0]
        if sk not in self.dsem:
            nm = 'dma_%d' % len(self.dsem)
            cm = self.nc.semaphore(nm)
            self._cms.append(cm)
            self.dsem[sk] = [cm.__enter__(), 0, nm]
            self.semobj[nm] = self.dsem[sk][0]
        return self.dsem[sk]

    def dma(self, e, out, in_, reads=(), writes=(), semkey=None, **kw):
        self._deps(e, reads, writes)
        sk = semkey if semkey is not None else (writes[0] if writes else reads[0])
        rec = self._dsem(sk)
        ins = self.eng[e].dma_start(out=out, in_=in_, **kw)
        rec[1] += 16
        ins.then_inc(rec[0], 16)
        tok = (rec[2], rec[1], None)
        self._commit(tok, reads, writes)
        return tok

    def idma(self, out, out_offset, in_, in_offset, reads=(), writes=(), semkey=None, **kw):
        e = 'pool'
        self._deps(e, reads, writes)
        sk = semkey if semkey is not None else (writes[0] if writes else reads[0])
        rec = self._dsem(sk)
        ins = self.nc.gpsimd.indirect_dma_start(out=out, out_offset=out_offset, in_=in_, in_offset=in_offset, **kw)
        rec[1] += 16
        ins.then_inc(rec[0], 16)
        tok = (rec[2], rec[1], None)
        self._commit(tok, reads, writes)
        return tok

    def wait_tok(self, e, tok):
        self._wait(e, tok)

    def finish(self, toks):
        for t in toks:
            self._wait('sp', t)


def K(t, sub=None):
    return (t.name, sub)


def _new_nc():
    return bass.Bass("TRN2", target_bir_lowering=False)


def _run(nc, in_maps):
    res = run_bass_kernel_spmd(nc, in_maps, core_ids=list(range(NCORES)))
    return res.results


def _bc(ap, shape):
    return ap.broadcast_to(shape)


def build_l0():
    nc = _new_nc()
    cT = nc.dram_tensor("cT", [128, 8, 3], F32, kind="ExternalInput")
    wm = nc.dram_tensor("wm", [2, 1024, 768], F32, kind="ExternalInput")
    bm = nc.dram_tensor("bm", [2, 3, 768], F32, kind="ExternalInput")
    mo = nc.dram_tensor("mo", [2, 3, 768], F32, kind="ExternalOutput")
    P = Prog(nc)
    cs = P.sb("cs", [128, 8, 3], F32)
    sc = P.sb("sc", [128, 8, 3], F32)
    ws = [P.sb("ws%d" % i, [128, 8, 768], F32) for i in range(2)]
    bs = P.sb("bs", [3, 2, 768], F32)
    os_ = P.sb("os", [3, 2, 768], F32)
    pm = [P.ps("pm%d" % i, [3, 512], F32) for i in range(4)]
    P.dma('sp', cs[:], cT[:, :, :], writes=[K(cs)])
    for l in range(2):
        P.dma('sp' if l == 0 else 'act', ws[l][:], wm[l].rearrange("(k p) n -> p k n", p=128), writes=[K(ws[l])])
    P.dma('pool', bs[:], bm.ap().rearrange("l v n -> v l n"), writes=[K(bs)])
    P.op('act', lambda: nc.scalar.activation(out=sc[:], in_=cs[:], func=AF.Silu), reads=[K(cs)], writes=[K(sc)])
    toks = []
    for l in range(2):
        for h in range(2):
            pt = pm[l * 2 + h]
            for k in range(8):
                P.op('pe', lambda: nc.tensor.matmul(pt[:, 0:384], lhsT=sc[:, k, :], rhs=ws[l][:, k, h * 384:(h + 1) * 384],
                                                    start=(k == 0), stop=(k == 7)),
                     reads=[K(sc), K(ws[l])], writes=[K(pt)])
            P.op('dve', lambda: nc.vector.tensor_tensor(out=os_[:, l, h * 384:(h + 1) * 384], in0=pt[:, 0:384],
                                                        in1=bs[:, l, h * 384:(h + 1) * 384], op=ALU.add),
                 reads=[K(pt), K(bs)], writes=[K(os_, (l, h))])
    toks.append(P.dma('sp', mo.ap().rearrange("l v n -> v l n"), os_[:], reads=[K(os_)], writes=[K(mo)]))
    P.finish(toks)
    P.close()
    return nc


def run_l0(c, c_ctx, w_mod, b_mod):
    vecs = np.stack([c[0], c[1], c_ctx], axis=0)
    cT = np.ascontiguousarray(vecs.reshape(3, 8, 128).transpose(2, 1, 0))
    nc = build_l0()
    in_maps = []
    for j in range(NCORES):
        sl = slice(j * 768, (j + 1) * 768)
        in_maps.append({
            "cT": cT,
            "wm": np.ascontiguousarray(w_mod[:, :, sl]),
            "bm": np.ascontiguousarray(np.broadcast_to(b_mod[:, None, sl], (2, 3, 768))),
        })
    res = _run(nc, in_maps)
    return np.concatenate([res[j]["mo"] for j in range(NCORES)], axis=2)


def emit_rsqrt_mean(P, nc, out, in_, kout, kin, n):
    P.op('dve', lambda: nc.vector.tensor_scalar(out=out, in0=in_, scalar1=1.0 / n, scalar2=EPS, op0=ALU.mult, op1=ALU.add),
         reads=[kin], writes=[kout])
    P.op('act', lambda: nc.scalar.activation(out=out, in_=out, func=AF.Sqrt), reads=[kout], writes=[kout])
    P.op('dve', lambda: nc.vector.reciprocal(out=out, in_=out), reads=[kout], writes=[kout])


def emit_norm_mod(P, nc, xt, kx, Ak, Bk, kind, hb, scr, ss, rstd, tagk):
    P.op('act', lambda: nc.scalar.activation(out=scr[:], in_=xt[:], func=AF.Square, accum_out=ss[:, 0:1]),
         reads=[kx], writes=[K(scr), K(ss)])
    emit_rsqrt_mean(P, nc, rstd[:, 0:1], ss[:, 0:1], K(rstd), K(ss), D)


def build_l1():
    nc = _new_nc()
    xw = nc.dram_tensor("xw", [ROWS, D], F32, kind="ExternalInput")
    msc = nc.dram_tensor("msc", [2, 128, D], F32, kind="ExternalInput")
    msh = nc.dram_tensor("msh", [2, 128, D], F32, kind="ExternalInput")
    n1w = nc.dram_tensor("n1w", [128, D], F32, kind="ExternalInput")
    win = nc.dram_tensor("win", [D, INW], F32, kind="ExternalInput")
    qkw = nc.dram_tensor("qkw", [128, 2, 64], F32, kind="ExternalInput")
    rope = nc.dram_tensor("rope", [ROWS, 2, 32], F32, kind="ExternalInput")
    pbd = nc.dram_tensor("pbd", [256, 256], F32, kind="ExternalInput")
    idn = nc.dram_tensor("idn", [128, 128], F32, kind="ExternalInput")
    o_qp = nc.dram_tensor("o_qp", [ROWS, 512], BF16, kind="ExternalOutput")
    o_qr = nc.dram_tensor("o_qr", [ROWS, 512], BF16, kind="ExternalOutput")
    o_kr = nc.dram_tensor("o_kr", [ROWS, 512], BF16, kind="ExternalOutput")
    o_v = nc.dram_tensor("o_v", [ROWS, 512], BF16, kind="ExternalOutput")
    o_hy = nc.dram_tensor("o_hy", [ROWS, 768], F32, kind="ExternalOutput")
    o_pv = nc.dram_tensor("o_pv", [ROWS, 256], F32, kind="ExternalOutput")
    P = Prog(nc)
    Ak = P.sb("Ak", [128, 2, D], F32)
    Bk = P.sb("Bk", [128, 2, D], F32)
    nw = P.sb("nw", [128, D], F32)
    wb = P.sb("wb", [128, 8, INW], BF16)
    wst = [P.sb("wst%d" % i, [128, INW], F32) for i in range(2)]
    qk = P.sb("qk", [128, 2, 64], F32)
    ids = P.sb("ids", [128, 128], F32)
    idb = P.sb("idb", [128, 128], BF16)
    bds = P.sb("bds", [128, 2, 256], F32)
    bdb = P.sb("bdb", [128, 2, 256], BF16)
    rp = P.sb("rp", [128, TPC, 64], F32)
    xt = [P.sb("xt%d" % i, [128, D], F32) for i in range(2)]
    scr = P.sb("scr", [128, D], BF16)
    ss = P.sb("ss", [128, 1], F32)
    rstd = P.sb("rstd", [128, 1], F32)
    h32 = P.sb("h32", [128, D], F32)
    hb = P.sb("hb", [128, D], BF16)
    hT = P.sb("hT", [128, 8, 128], BF16)
    sq = P.sb("sq", [128, 512], F32)
    ssq = P.sb("ssq", [128, 8], F32)
    qn = [P.sb("qn%d" % i, [128, 512], F32) for i in range(2)]
    t1 = [P.sb("t1_%d" % i, [128, 256], F32) for i in range(2)]
    t2 = [P.sb("t2_%d" % i, [128, 256], F32) for i in range(2)]
    oqp = P.sb("oqp", [128, 512], BF16)
    orot = [P.sb("orot%d" % i, [128, 512], BF16) for i in range(2)]
    ov = P.sb("ov", [128, 512], BF16)
    ohy = P.sb("ohy", [128, 768], F32)
    upT = P.sb("upT", [128, 2, 128], BF16)
    opv = P.sb("opv", [128, 256], F32)
    pT = P.ps("pT", [128, 8, 128], BF16)
    pg = [P.ps("pg%d" % i, [128, 512], F32) for i in range(5)]
    ppT = P.ps("ppT", [128, 2, 256], F32)
    ppv = P.ps("ppv", [128, 512], F32)

    P.dma('sp', nw[:], n1w[:, :], writes=[K(nw)])
    P.dma('act', Ak[:], msc.ap().rearrange("k p d -> p k d"), writes=[K(Ak)])
    P.dma('pool', Bk[:], msh.ap().rearrange("k p d -> p k d"), writes=[K(Bk)])
    P.dma('sp', qk[:], qkw[:, :, :], writes=[K(qk)])
    P.dma('sp', ids[:], idn[:, :], writes=[K(ids)])
    P.dma('act', bds[:], pbd.ap().rearrange("(j p) n -> p j n", p=128), writes=[K(bds)])
    P.dma('pool', rp[:], rope.ap().rearrange("(t p) a f -> p t (a f)", p=128), writes=[K(rp)])
    P.op('act', lambda: nc.scalar.copy(out=idb[:], in_=ids[:]), reads=[K(ids)], writes=[K(idb)])
    P.op('act', lambda: nc.scalar.copy(out=bdb[:], in_=bds[:]), reads=[K(bds)], writes=[K(bdb)])
    for k in range(2):
        P.op('dve', lambda: nc.vector.scalar_tensor_tensor(out=Ak[:, k, :], in0=Ak[:, k, :], scalar=1.0, in1=nw[:],
                                                           op0=ALU.add, op1=ALU.mult),
             reads=[K(Ak), K(nw)], writes=[K(Ak)])
    for c in range(8):
        st = wst[c % 2]
        P.dma('sp' if c % 2 == 0 else 'act', st[:], win[c * 128:(c + 1) * 128, :], writes=[K(st)])
        if c % 2 == 0:
            P.op('dve', lambda: nc.vector.tensor_copy(out=wb[:, c, :], in_=st[:]), reads=[K(st)], writes=[K(wb, c)])
        else:
            P.op('pool', lambda: nc.gpsimd.tensor_copy(out=wb[:, c, :], in_=st[:]), reads=[K(st)], writes=[K(wb, c)])

    outs = []
    for t in range(TPC):
        kind = 1 if t == TPC - 1 else 0
        x_ = xt[t % 2]
        P.dma('sp', x_[:], xw[t * 128:(t + 1) * 128, :], writes=[K(x_)])
        emit_norm_mod(P, nc, x_, K(x_), Ak, Bk, kind, hb, scr, ss, rstd, t)
        P.op('dve', lambda: nc.vector.scalar_tensor_tensor(out=h32[:], in0=x_[:], scalar=rstd[:, 0:1], in1=Ak[:, kind, :],
                                                           op0=ALU.mult, op1=ALU.mult),
             reads=[K(x_), K(rstd), K(Ak)], writes=[K(h32)])
        P.op('pool', lambda: nc.gpsimd.tensor_tensor(out=hb[:], in0=h32[:], in1=Bk[:, kind, :], op=ALU.add),
             reads=[K(h32), K(Bk)], writes=[K(hb)])
        for c in range(8):
            P.op('pe', lambda: nc.tensor.transpose(out=pT[:, c, :], in_=hb[:, c * 128:(c + 1) * 128], identity=idb[:]),
                 reads=[K(hb), K(idb)], writes=[K(pT)])
        P.op('act', lambda: nc.scalar.copy(out=hT[:], in_=pT[:]), reads=[K(pT)], writes=[K(hT)])
        for g in range(5):
            ncol = 512 if g < 4 else 256
            for c in range(8):
                P.op('pe', lambda: nc.tensor.matmul(pg[g][:, 0:ncol], lhsT=hT[:, c, :], rhs=wb[:, c, g * 512:g * 512 + ncol],
                                                    start=(c == 0), stop=(c == 7)),
                     reads=[K(hT), K(wb)], writes=[K(pg[g])])
        for j in range(2):
            for c in range(8):
                P.op('pe', lambda: nc.tensor.matmul(ppT[:, j, 0:128], lhsT=wb[:, c, 2304 + j * 128:2304 + (j + 1) * 128],
                                                    rhs=hT[:, c, :], start=(c == 0), stop=(c == 7)),
                     reads=[K(hT), K(wb)], writes=[K(ppT)])
        P.op('act', lambda: nc.scalar.copy(out=upT[:], in_=ppT[:, :, 0:128]), reads=[K(ppT)], writes=[K(upT)])
        for j in range(2):
            P.op('pe', lambda: nc.tensor.matmul(ppv[:, 0:256], lhsT=upT[:, j, :], rhs=bdb[:, j, :], start=(j == 0), stop=(j == 1)),
                 reads=[K(upT), K(bdb)], writes=[K(ppv)])
        P.op('act', lambda: nc.scalar.copy(out=opv[:], in_=ppv[:, 0:256]), reads=[K(ppv)], writes=[K(opv)])
        outs.append(P.dma('act', o_pv[t * 128:(t + 1) * 128, :], opv[:], reads=[K(opv)], writes=[K(o_pv, t)]))
        for qi in range(2):
            src = pg[qi]
            q_ = qn[qi]
            P.op('act', lambda: nc.scalar.activation(out=sq[:], in_=src[:], func=AF.Square), reads=[K(src)], writes=[K(sq)])
            P.op('dve', lambda: nc.vector.tensor_reduce(out=ssq[:, :], in_=sq[:].rearrange("p (h e) -> p h e", h=8), axis=AX.X,
                                                        op=ALU.add), reads=[K(sq)], writes=[K(ssq)])
            emit_rsqrt_mean(P, nc, ssq[:, :], ssq[:, :], K(ssq), K(ssq), 64)
            q3 = q_[:].rearrange("p (h e) -> p h e", h=8)
            P.op('dve', lambda: nc.vector.tensor_tensor(out=q3, in0=src[:].rearrange("p (h e) -> p h e", h=8),
                                                        in1=_bc(ssq[:, :].unsqueeze(2), [128, 8, 64]), op=ALU.mult),
                 reads=[K(src), K(ssq)], writes=[K(q_)])
            P.op('pool', lambda: nc.gpsimd.tensor_tensor(out=q3, in0=q3, in1=_bc(qk[:, qi, :].unsqueeze(1), [128, 8, 64]),
                                                         op=ALU.mult), reads=[K(q_), K(qk)], writes=[K(q_)])
            if qi == 0:
                P.op('act', lambda: nc.scalar.copy(out=oqp[:], in_=q_[:]), reads=[K(q_)], writes=[K(oqp)])
                outs.append(P.dma('sp', o_qp[t * 128:(t + 1) * 128, :], oqp[:], reads=[K(oqp)], writes=[K(o_qp, t)]))
            q5 = q_[:].rearrange("p (h a b f) -> p h a b f", h=8, a=2, b=2)
            o5 = orot[qi][:].rearrange("p (h a b f) -> p h a b f", h=8, a=2, b=2)
            a_ = q5[:, :, :, 0, :]
            b_ = q5[:, :, :, 1, :]
            cs_ = _bc(rp[:, t, 0:32].rearrange("p (a f) -> p a f", a=2).unsqueeze(1), [128, 8, 2, 16])
            sn_ = _bc(rp[:, t, 32:64].rearrange("p (a f) -> p a f", a=2).unsqueeze(1), [128, 8, 2, 16])
            v1 = t1[qi][:].rearrange("p (h a f) -> p h a f", h=8, a=2)
            v2 = t2[qi][:].rearrange("p (h a f) -> p h a f", h=8, a=2)
            e1, e2 = ('dve', 'pool') if qi == 0 else ('pool', 'dve')
            E1, E2 = P.eng[e1], P.eng[e2]
            P.op(e1, lambda: E1.tensor_tensor(out=v1, in0=a_, in1=cs_, op=ALU.mult), reads=[K(q_), K(rp)], writes=[K(t1[qi])])
            P.op(e2, lambda: E2.tensor_tensor(out=v2, in0=b_, in1=sn_, op=ALU.mult), reads=[K(q_), K(rp)], writes=[K(t2[qi])])
            P.op(e1, lambda: E1.tensor_tensor(out=o5[:, :, :, 0, :], in0=v1, in1=v2, op=ALU.subtract),
                 reads=[K(t1[qi]), K(t2[qi])], writes=[K(orot[qi], 'a')])
            P.op(e1, lambda: E1.tensor_tensor(out=v1, in0=b_, in1=cs_, op=ALU.mult), reads=[K(q_), K(rp)], writes=[K(t1[qi])])
            P.op(e2, lambda: E2.tensor_tensor(out=v2, in0=a_, in1=sn_, op=ALU.mult), reads=[K(q_), K(rp)], writes=[K(t2[qi])])
            P.op(e2, lambda: E2.tensor_tensor(out=o5[:, :, :, 1, :], in0=v1, in1=v2, op=ALU.add),
                 reads=[K(t1[qi]), K(t2[qi])], writes=[K(orot[qi], 'b')])
            dst = o_qr if qi == 0 else o_kr
            outs.append(P.dma('sp', dst[t * 128:(t + 1) * 128, :], orot[qi][:], reads=[K(orot[qi])], writes=[K(dst, t)]))
        P.op('act', lambda: nc.scalar.copy(out=ov[:], in_=pg[2][:]), reads=[K(pg[2])], writes=[K(ov)])
        outs.append(P.dma('act', o_v[t * 128:(t + 1) * 128, :], ov[:], reads=[K(ov)], writes=[K(o_v, t)]))
        P.op('act', lambda: nc.scalar.copy(out=ohy[:, 0:512], in_=pg[3][:]), reads=[K(pg[3])], writes=[K(ohy, 0)])
        P.op('dve', lambda: nc.vector.tensor_copy(out=ohy[:, 512:768], in_=pg[4][:, 0:256]), reads=[K(pg[4])], writes=[K(ohy, 1)])
        outs.append(P.dma('act', o_hy[t * 128:(t + 1) * 128, :], ohy[:], reads=[K(ohy)], writes=[K(o_hy, t)]))
    P.finish(outs)
    P.close()
    return nc


def rope_tables():
    t = np.arange(N)
    pos = np.stack([t // GW, t % GW], axis=-1).astype(np.float32)
    nf = 16
    inv = (10000.0 ** (-np.arange(nf, dtype=np.float32) / nf)).astype(np.float32)
    ang = pos[:, :, None] * inv
    return np.cos(ang).astype(np.float32), np.sin(ang).astype(np.float32)


def core_rows(j):
    b = j // 4
    t0 = (j % 4) * 2048
    cb = (j % 4) // 2
    ct0 = (j % 2) * 128
    return b, t0, cb, ct0


def rep128(v):
    return np.ascontiguousarray(np.broadcast_to(v[None, :], (128,) + v.shape))


def run_l1(xl, xc, mod, lyr, p):
    nc = build_l1()
    cosr, sinr = rope_tables()
    bd = np.zeros((256, 256), np.float32)
    for g in range(4):
        bd[g * 64:(g + 1) * 64, g * 64:(g + 1) * 64] = p['pool_w'][g]
    in_maps = []
    for j in range(NCORES):
        b, t0, cb, ct0 = core_rows(j)
        xw = np.concatenate([xl[b, t0:t0 + 2048], xc[cb, ct0:ct0 + 128]], axis=0)
        rope = np.zeros((ROWS, 2, 32), np.float32)
        rope[:2048, 0] = cosr[t0:t0 + 2048].reshape(2048, 32)
        rope[:2048, 1] = sinr[t0:t0 + 2048].reshape(2048, 32)
        rope[2048:, 0] = 1.0
        msc = np.stack([rep128(mod[b, D:2 * D]), rep128(mod[2, D:2 * D])])
        msh = np.stack([rep128(mod[b, 0:D]), rep128(mod[2, 0:D])])
        in_maps.append({
            "xw": np.ascontiguousarray(xw), "msc": msc, "msh": msh, "n1w": rep128(p['norm1_w']),
            "win": np.ascontiguousarray(p['w_in']),
            "qkw": np.ascontiguousarray(np.broadcast_to(np.stack([p['q_norm_w'], p['k_norm_w']])[None], (128, 2, 64))),
            "rope": rope, "pbd": bd, "idn": np.eye(128, dtype=np.float32),
        })
    res = _run(nc, in_maps)
    out = {}
    for name in ["o_qp", "o_qr", "o_kr", "o_v", "o_hy", "o_pv"]:
        full = np.stack([res[j][name] for j in range(NCORES)])
        lat = full[:, :2048].reshape(B, N, -1)
        cx = np.stack([np.concatenate([full[2 * cb + h, 2048:] for h in range(2)], axis=0) for cb in range(B)])
        out[name] = (lat, cx)
    return out


W0S = [-4, 28, 60, 92]


def build_l2a(with_ctx, dbg_hp=4, dbg_blk=16, dbg_h=2, dbg_stage=9):
    nc = _new_nc()
    qrT = nc.dram_tensor("qrT", [4, 128, 2048], BF16, kind="ExternalInput")
    qpT = nc.dram_tensor("qpT", [4, 128, 2048], BF16, kind="ExternalInput")
    kT = nc.dram_tensor("kT", [4, 128, 2560], BF16, kind="ExternalInput")
    vv = nc.dram_tensor("vv", [2560, 512], BF16, kind="ExternalInput")
    kcT = nc.dram_tensor("kcT", [4, 128, 256], BF16, kind="ExternalInput")
    vc = nc.dram_tensor("vc", [256, 512], BF16, kind="ExternalInput")
    qcT = nc.dram_tensor("qcT", [4, 128, 128], BF16, kind="ExternalInput")
    kc2T = nc.dram_tensor("kc2T", [4, 128, 256], BF16, kind="ExternalInput")
    vc2 = nc.dram_tensor("vc2", [256, 512], BF16, kind="ExternalInput")
    bias = nc.dram_tensor("bias", [4, 128, 2, 5, 640], F32, kind="ExternalInput")
    idn = nc.dram_tensor("idn", [128, 128], F32, kind="ExternalInput")
    o_na = nc.dram_tensor("o_na", [2048, 512], F32, kind="ExternalOutput")
    o_c = nc.dram_tensor("o_c", [128, 512], F32, kind="ExternalOutput")
    P = Prog(nc)
    s_qr = P.sb("s_qr", [128, 4, 2048], BF16)
    s_qp = P.sb("s_qp", [128, 4, 2048], BF16)
    s_k = P.sb("s_k", [128, 4, 2560], BF16)
    s_v = P.sb("s_v", [128, 20, 512], BF16)
    s_kc = P.sb("s_kc", [128, 4, 256], BF16)
    s_vc = P.sb("s_vc", [128, 2, 512], BF16)
    s_qc = P.sb("s_qc", [128, 4, 128], BF16)
    s_kc2 = P.sb("s_kc2", [128, 4, 256], BF16)
    s_vc2 = P.sb("s_vc2", [128, 2, 512], BF16)
    s_b = P.sb("s_b", [128, 2, 5, 640], F32)
    ids = P.sb("ids", [128, 128], F32)
    idb = P.sb("idb", [128, 128], BF16)
    sc = [P.sb("sc%d" % i, [128, 896], F32) for i in range(2)]
    pb = [P.sb("pb%d" % i, [128, 896], BF16) for i in range(2)]
    pts = [P.sb("pts%d" % i, [128, 7, 128], BF16) for i in range(2)]
    sm = [P.sb("sm%d" % i, [128, 4], F32) for i in range(2)]
    ona = P.sb("ona", [128, 16, 512], F32)
    oc = P.sb("oc", [128, 512], F32)
    pS = [P.ps("pS%d" % i, [128, 1024], F32) for i in range(2)]
    pPT = [P.ps("pPT%d" % i, [128, 8, 128], BF16) for i in range(2)]
    pO = [P.ps("pO%d" % i, [128, 512], F32) for i in range(2)]

    P.dma('sp', s_qr[:], qrT.ap().rearrange("h p t -> p h t"), writes=[K(s_qr)])
    P.dma('act', s_qp[:], qpT.ap().rearrange("h p t -> p h t"), writes=[K(s_qp)])
    P.dma('pool', s_k[:], kT.ap().rearrange("h p t -> p h t"), writes=[K(s_k)])
    P.dma('sp', s_v[:], vv.ap().rearrange("(i p) d -> p i d", p=128), writes=[K(s_v)])
    P.dma('act', s_kc[:], kcT.ap().rearrange("h p t -> p h t"), writes=[K(s_kc)])
    P.dma('act', s_vc[:], vc.ap().rearrange("(i p) d -> p i d", p=128), writes=[K(s_vc)])
    P.dma('pool', s_qc[:], qcT.ap().rearrange("h p t -> p h t"), writes=[K(s_qc)])
    P.dma('pool', s_kc2[:], kc2T.ap().rearrange("h p t -> p h t"), writes=[K(s_kc2)])
    P.dma('pool', s_vc2[:], vc2.ap().rearrange("(i p) d -> p i d", p=128), writes=[K(s_vc2)])
    P.dma('sp', ids[:], idn[:, :], writes=[K(ids)])
    P.op('act', lambda: nc.scalar.copy(out=idb[:], in_=ids[:]), reads=[K(ids)], writes=[K(idb)])

    cnt = [0]

    def unit(hp, h, lat, blk, dst, kdst):
        i = cnt[0] % 2
        cnt[0] += 1
        S, SC, PB, PTS, SM, PT, O = pS[i], sc[i], pb[i], pts[i], sm[i], pPT[i], pO[i]
        pr = slice(64 * h, 64 * h + 64)
        hd = 2 * hp + h
        if lat:
            kt0 = blk
            slot = 0 if blk == 0 else 1 if blk == 1 else 3 if blk == 14 else 4 if blk == 15 else 2
            qs = slice(blk * 128, (blk + 1) * 128)
            P.op('pe', lambda: nc.tensor.matmul(S[:, 0:512], lhsT=s_qr[pr, hp, qs], rhs=s_k[pr, hp, kt0 * 128:kt0 * 128 + 512],
                                                start=True, stop=True), reads=[K(s_qr), K(s_k)], writes=[K(S)])
            P.op('pe', lambda: nc.tensor.matmul(S[:, 512:640], lhsT=s_qr[pr, hp, qs],
                                                rhs=s_k[pr, hp, kt0 * 128 + 512:kt0 * 128 + 640], start=True, stop=True),
                 reads=[K(s_qr), K(s_k)], writes=[K(S)])
            P.op('pe', lambda: nc.tensor.matmul(S[:, 640:896], lhsT=s_qp[pr, hp, qs], rhs=s_kc[pr, hp, :], start=True, stop=True),
                 reads=[K(s_qp), K(s_kc)], writes=[K(S)])
            for (c0, c1) in ((0, 512), (512, 640)):
                P.op('dve', lambda: nc.vector.scalar_tensor_tensor(out=SC[:, c0:c1], in0=S[:, c0:c1], scalar=0.125,
                                                                   in1=s_b[:, h, slot, c0:c1], op0=ALU.mult, op1=ALU.add),
                     reads=[K(S), K(s_b)], writes=[K(SC)])
            P.op('act', lambda: nc.scalar.mul(out=SC[:, 640:896], in_=S[:, 640:896], mul=0.125), reads=[K(S)], writes=[K(SC)])
            lo, nt = 0, 7
            if dbg_stage < 2:
                return
        else:
            P.op('pe', lambda: nc.tensor.matmul(S[:, 640:896], lhsT=s_qc[pr, hp, :], rhs=s_kc2[pr, hp, :], start=True, stop=True),
                 reads=[K(s_qc), K(s_kc2)], writes=[K(S)])
            P.op('act', lambda: nc.scalar.mul(out=SC[:, 640:896], in_=S[:, 640:896], mul=0.125), reads=[K(S)], writes=[K(SC)])
            lo, nt = 640, 2
        P.op('dve', lambda: nc.vector.tensor_reduce(out=SM[:, 0:1], in_=SC[:, lo:896], axis=AX.X, op=ALU.max),
             reads=[K(SC)], writes=[K(SM)])
        P.op('dve', lambda: nc.vector.tensor_scalar(out=SM[:, 1:2], in0=SM[:, 0:1], scalar1=-1.0, scalar2=None, op0=ALU.mult),
             reads=[K(SM)], writes=[K(SM)])
        if dbg_stage < 4:
            return
        P.op('act', lambda: nc.scalar.activation(out=PB[:, lo:896], in_=SC[:, lo:896], func=AF.Exp, bias=SM[:, 1:2], scale=1.0,
                                                 accum_out=SM[:, 2:3]), reads=[K(SC), K(SM)], writes=[K(PB), K(SM)])
        if dbg_stage < 5:
            return
        for jj in range(nt):
            c0 = lo + jj * 128
            P.op('pe', lambda: nc.tensor.transpose(out=PT[:, jj, :], in_=PB[:, c0:c0 + 128], identity=idb[:]),
                 reads=[K(PB), K(idb)], writes=[K(PT)])
        if dbg_stage < 6:
            return
        P.op('act', lambda: nc.scalar.copy(out=PTS[:, 0:nt, :], in_=PT[:, 0:nt, :]), reads=[K(PT)], writes=[K(PTS)])
        if dbg_stage < 7:
            return
        for jj in range(nt):
            if lat:
                rhs = s_v[:, blk + jj, hd * 64:(hd + 1) * 64] if jj < 5 else s_vc[:, jj - 5, hd * 64:(hd + 1) * 64]
            else:
                rhs = s_vc2[:, jj, hd * 64:(hd + 1) * 64]
            P.op('pe', lambda: nc.tensor.matmul(O[:, 0:64], lhsT=PTS[:, jj, :], rhs=rhs, start=(jj == 0), stop=(jj == nt - 1)),
                 reads=[K(PTS), K(s_v), K(s_vc), K(s_vc2)], writes=[K(O)])
        if dbg_stage < 8:
            return
        P.op('dve', lambda: nc.vector.reciprocal(out=SM[:, 3:4], in_=SM[:, 2:3]), reads=[K(SM)], writes=[K(SM)])
        P.op('act', lambda: nc.scalar.activation(out=dst, in_=O[:, 0:64], func=AF.Copy, scale=SM[:, 3:4]),
             reads=[K(O), K(SM)], writes=[kdst])

    for hp in range(dbg_hp):
        P.dma('sp', s_b[:], bias[hp], writes=[K(s_b)])
        for blk in range(dbg_blk):
            for h in range(dbg_h):
                hd = 2 * hp + h
                unit(hp, h, True, blk, ona[:, blk, hd * 64:(hd + 1) * 64], K(ona, (blk, hd)))
        if with_ctx:
            for h in range(2):
                hd = 2 * hp + h
                unit(hp, h, False, 0, oc[:, hd * 64:(hd + 1) * 64], K(oc, hd))
    outs = [P.dma('sp', o_na.ap().rearrange("(i p) d -> p i d", p=128), ona[:], reads=[K(ona)], writes=[K(o_na)])]
    if with_ctx:
        outs.append(P.dma('act', o_c[:, :], oc[:], reads=[K(oc)], writes=[K(o_c)]))
    P.finish(outs)
    P.close()
    return nc


def na_bias_tables(rpb):
    out = {}
    qi = np.arange(128)
    kk = np.arange(640)
    for m in [0, 1, 2, 62, 63]:
        kb2 = 2 * m - 4
        r = 2 * m + qi // 64
        c = qi % 64
        kr = kb2 + kk // 64
        kc = kk % 64
        rs = np.clip(r - 4, 0, 120)
        cs = np.clip(c - 8, 0, 48)
        valid = ((kr[None, :] >= rs[:, None]) & (kr[None, :] < rs[:, None] + 8) &
                 (kc[None, :] >= cs[:, None]) & (kc[None, :] < cs[:, None] + 16))
        ri = np.clip(kr[None, :] - r[:, None] + 7, 0, 14)
        ci = np.clip(kc[None, :] - c[:, None] + 15, 0, 30)
        g = rpb[:, ri, ci]
        out[m] = np.where(valid[None], g, np.float32(-30000.0)).astype(np.float32)
    return out


def tT(a):
    T = a.shape[0]
    return np.ascontiguousarray(a.reshape(T, 4, 128).transpose(1, 2, 0))


def _win(a, w0):
    out = np.zeros((2560, a.shape[1]), a.dtype)
    lo, hi = max(w0, 0), min(w0 + 2560, a.shape[0])
    out[lo - w0:hi - w0] = a[lo:hi]
    return out


def run_l2a(o1, rpb, with_ctx):
    nc = build_l2a(with_ctx)
    bt = na_bias_tables(rpb)
    in_maps = []
    for j in range(NCORES):
        b, t0, cb, ct0 = core_rows(j)
        q = j % 4
        w0 = W0S[q] * 64
        slots = []
        for s, blk in enumerate([0, 1, 2, 14, 15]):
            m = 16 * q + blk
            mt = m if m in (0, 1, 62, 63) else 2
            slots.append(bt[mt])
        bs = np.stack(slots, axis=0)
        bs = bs.reshape(5, 4, 2, 128, 640).transpose(1, 3, 2, 0, 4)
        in_maps.append({
            "qrT": tT(o1["o_qr"][0][b, t0:t0 + 2048]), "qpT": tT(o1["o_qp"][0][b, t0:t0 + 2048]),
            "kT": tT(_win(o1["o_kr"][0][b], w0)), "vv": _win(o1["o_v"][0][b], w0),
            "kcT": tT(o1["o_kr"][1][b]), "vc": np.ascontiguousarray(o1["o_v"][1][b]),
            "qcT": tT(o1["o_qp"][1][cb, ct0:ct0 + 128]), "kc2T": tT(o1["o_kr"][1][cb]),
            "vc2": np.ascontiguousarray(o1["o_v"][1][cb]),
            "bias": np.ascontiguousarray(bs), "idn": np.eye(128, dtype=np.float32),
        })
    res = _run(nc, in_maps)
    o_na = np.stack([res[j]["o_na"] for j in range(NCORES)]).reshape(B, N, 512)
    o_c = None
    if with_ctx:
        o_c = np.stack([np.concatenate([res[2 * cb + h]["o_c"] for h in range(2)], axis=0) for cb in range(B)])
    return o_na, o_c


NFFT = 16384
TWO_PI = 2.0 * math.pi


def hyena_consts():
    n = np.arange(128)
    ang = TWO_PI * np.outer(n, n) / 128.0
    Fr, Fi = np.cos(ang), -np.sin(ang)
    cm = np.stack([Fr, Fi, Fr, -Fi, Fi, Fr], axis=1).astype(np.float32)
    ang2 = TWO_PI * np.outer(n, n) / NFFT
    tw = np.stack([np.cos(ang2), -np.sin(ang2)], axis=1).astype(np.float32)
    return cm, tw


def hyena_tap_tables(L, chans):
    nn = np.arange(NFFT)
    fwd = nn < L
    bwd = nn > NFFT - L
    tpos = np.where(fwd, nn, np.where(bwd, NFFT - nn, 0)).astype(np.float64)
    t = (tpos / (L - 1)).astype(np.float32)
    w = ((2.0 * math.pi / L) * tpos).astype(np.float32)
    f = np.linspace(1e-4, 15, 16, dtype=np.float32)
    z = np.concatenate([t[:, None], np.cos(f[None, :] * w[:, None]), -np.sin(f[None, :] * w[:, None])], axis=1)
    zT = z.reshape(128, 128, 33).transpose(2, 1, 0).reshape(33, NFFT)
    deltas = np.abs(np.linspace(math.log(1e-2) / 1.5, math.log(1e-2) / 0.3, 256, dtype=np.float32))[chans]
    dec = np.exp(-t[:, None] * deltas[None, :]).astype(np.float32)
    E = np.zeros((NFFT, 2, len(chans)), np.float32)
    E[fwd, 0] = dec[fwd]
    E[bwd, 1] = dec[bwd]
    E = E.reshape(128, 128, 2, len(chans))
    return np.ascontiguousarray(zT.astype(np.float32)), np.ascontiguousarray(E)


def build_l2b(nsets_ctx):
    nc = _new_nc()
    HC = 16
    cm_d = nc.dram_tensor("cm", [128, 6, 128], F32, kind="ExternalInput")
    tw_d = nc.dram_tensor("tw", [128, 2, 128], F32, kind="ExternalInput")
    zT_d = nc.dram_tensor("zT", [2, 33, NFFT], F32, kind="ExternalInput")
    E_d = nc.dram_tensor("E", [2, 2, 128, 128 * 2 * HC], F32, kind="ExternalInput")
    w1_d = nc.dram_tensor("w1", [33, 64], F32, kind="ExternalInput")
    w2_d = nc.dram_tensor("w2", [64, 64], F32, kind="ExternalInput")
    w3_d = nc.dram_tensor("w3", [2, 64, 2 * 2 * HC], F32, kind="ExternalInput")
    fb_d = nc.dram_tensor("fb", [64, 4], F32, kind="ExternalInput")
    U_d = nc.dram_tensor("U", [4, 2, 3, 64, 3 * HC * 128], F32, kind="ExternalInput")
    cw_d = nc.dram_tensor("cw", [2, 64, 3 * 3 * HC + 3 * HC + 2 * HC], F32, kind="ExternalInput")
    o_d = nc.dram_tensor("o_hy", [4, 2, 64, HC * 128], F32, kind="ExternalOutput")
    P = Prog(nc)
    cmf = P.sb("cmf", [128, 6, 128], F32)
    cmb = P.sb("cmb", [128, 6, 128], BF16)
    tws = P.sb("tws", [128, 2, 128], F32)
    w1s = P.sb("w1s", [33, 64], F32)
    w2s = P.sb("w2s", [64, 64], F32)
    w3s = P.sb("w3s", [64, 64], F32)
    fbs = P.sb("fbs", [64, 4], F32)
    fb2 = P.sb("fb2", [64, 4], F32)
    ones = P.sb("ones", [128, 128], F32)
    big = P.sb("big", [128, 6144], F32)
    Hs = P.sb("Hs", [128, 2 * HC, 256], F32)
    k2 = P.sb("k2", [128, 2 * HC, 128], F32)
    k2b = P.sb("k2b", [128, 2 * HC, 128], BF16)
    zts = [P.sb("zts%d" % i, [33, 512], F32) for i in range(2)]
    ar = [P.sb("ar%d" % i, [64, 512], F32) for i in range(2)]
    ai = P.sb("ai", [64, 512], I32)
    h1 = P.sb("h1", [64, 512], F32)
    h2 = P.sb("h2", [64, 512], F32)
    tmpE = P.sb("tmpE", [128, 4 * 64], F32)
    ssum = P.sb("ssum", [128, 2 * HC], F32)
    rn = P.sb("rn", [128, 2 * HC], F32)
    cws = P.sb("cws", [64, 3 * 3 * HC + 3 * HC + 2 * HC], F32)
    zs = [P.sb("zs%d" % i, [64, HC, 128], F32) for i in range(3)]
    zc = P.sb("zc", [64, HC, 128], F32)
    zb = P.sb("zb", [64, HC, 128], BF16)
    ct1 = P.sb("ct1", [64, HC, 128], F32)
    ct2 = P.sb("ct2", [64, HC, 128], F32)
    B3 = P.sb("B3", [128, HC, 384], BF16)
    Yb = P.sb("Yb", [128, HC, 256], BF16)
    Dr = P.sb("Dr", [128, HC, 128], BF16)
    Di = P.sb("Di", [128, HC, 128], BF16)
    T1 = [P.sb("T1_%d" % i, [128, 2, 256], F32) for i in range(2)]
    T2 = [P.sb("T2_%d" % i, [128, 2, 256], F32) for i in range(2)]
    pb = [P.ps("pb%d" % i, [128, 512], F32) for i in range(8)]

    P.dma('sp', cmf[:], cm_d[:, :, :], writes=[K(cmf)])
    P.dma('sp', tws[:], tw_d[:, :, :], writes=[K(tws)])
    P.dma('act', w1s[:], w1_d[:, :], writes=[K(w1s)])
    P.dma('act', w2s[:], w2_d[:, :], writes=[K(w2s)])
    P.dma('act', fbs[:], fb_d[:, :], writes=[K(fbs)])
    P.op('act', lambda: nc.scalar.copy(out=cmb[:], in_=cmf[:]), reads=[K(cmf)], writes=[K(cmb)])
    P.op('pool', lambda: nc.gpsimd.memset(ones[:], 1.0), writes=[K(ones)])
    P.op('dve', lambda: nc.vector.tensor_tensor(out=fb2[:, 0:1], in0=fbs[:, 0:1], in1=fbs[:, 1:2], op=ALU.mult), reads=[K(fbs)], writes=[K(fb2, 0)])
    P.op('dve', lambda: nc.vector.tensor_tensor(out=fb2[:, 1:2], in0=fbs[:, 0:1], in1=fbs[:, 2:3], op=ALU.mult), reads=[K(fbs)], writes=[K(fb2, 1)])
    P.op('pool', lambda: nc.gpsimd.memset(fb2[:, 2:3], -math.pi), writes=[K(fb2, 2)])
    Fr_l, Fi_l = cmb[:, 0, :], cmb[:, 1, :]
    rhsA = cmb[:, 0:2, :]
    rhsB = cmb[:, 2:4, :]
    rhsC = cmb[:, 4:6, :]
    twr = tws[:, 0, :]
    twi = tws[:, 1, :]
    pbi = [0]

    def nextpb():
        pbi[0] = (pbi[0] + 1) % 8
        return pb[pbi[0]]

    def sin_layer(src_ps, dst, bias_col):
        a0, a1 = ar
        P.op('act', lambda: nc.scalar.activation(out=a0[:], in_=src_ps, func=AF.Identity, bias=fb2[:, bias_col:bias_col + 1],
                                                 scale=fbs[:, 0:1]), reads=[K(src_ps.tensor), K(fb2), K(fbs)], writes=[K(a0)])
        P.op('dve', lambda: nc.vector.tensor_scalar(out=a0[:], in0=a0[:], scalar1=1.0 / TWO_PI, scalar2=16.5, op0=ALU.mult, op1=ALU.add),
             reads=[K(a0)], writes=[K(a0)])
        P.op('dve', lambda: nc.vector.tensor_copy(out=ai[:], in_=a0[:]), reads=[K(a0)], writes=[K(ai)])
        P.op('dve', lambda: nc.vector.tensor_copy(out=a1[:], in_=ai[:]), reads=[K(ai)], writes=[K(a1)])
        P.op('dve', lambda: nc.vector.tensor_tensor(out=a0[:], in0=a0[:], in1=a1[:], op=ALU.subtract), reads=[K(a0), K(a1)], writes=[K(a0)])
        P.op('dve', lambda: nc.vector.tensor_scalar(out=a1[:], in0=a0[:], scalar1=0.0, scalar2=None, op0=ALU.is_lt), reads=[K(a0)], writes=[K(a1)])
        P.op('dve', lambda: nc.vector.tensor_tensor(out=a0[:], in0=a0[:], in1=a1[:], op=ALU.add), reads=[K(a0), K(a1)], writes=[K(a0)])
        P.op('act', lambda: nc.scalar.activation(out=dst[:], in_=a0[:], func=AF.Sin, bias=fb2[:, 2:3], scale=TWO_PI),
             reads=[K(a0), K(fb2)], writes=[K(dst)])

    def fwd_fft(src_bf, n1u, nser, evac):
        for s0 in range(0, nser, 2):
            A = nextpb()
            for s in range(2):
                P.op('pe', lambda: nc.tensor.matmul(A[:, s * 256:(s + 1) * 256], lhsT=src_bf[0:n1u, s0 + s, :], rhs=rhsA[0:n1u],
                                                    start=True, stop=True), reads=[K(src_bf), K(cmb)], writes=[K(A)])
            i = (s0 // 2) % 2
            A4 = A[:].rearrange("p (s r k) -> p s r k", s=2, r=2)
            t1 = T1[i][:].rearrange("p s (r k) -> p s r k", r=2)
            t2 = T2[i][:].rearrange("p s (r k) -> p s r k", r=2)
            P.op('dve', lambda: nc.vector.tensor_tensor(out=t1, in0=A4, in1=_bc(twr.unsqueeze(1).unsqueeze(1), [128, 2, 2, 128]), op=ALU.mult),
                 reads=[K(A), K(tws)], writes=[K(T1[i])])
            P.op('dve', lambda: nc.vector.tensor_tensor(out=t2, in0=A4, in1=_bc(twi.unsqueeze(1).unsqueeze(1), [128, 2, 2, 128]), op=ALU.mult),
                 reads=[K(A), K(tws)], writes=[K(T2[i])])
            b0 = s0 % HC
            bs = B3[:, b0:b0 + 2, :]
            P.op('pool', lambda: nc.gpsimd.tensor_tensor(out=bs[:, :, 128:256], in0=t1[:, :, 0, :], in1=t2[:, :, 1, :], op=ALU.subtract),
                 reads=[K(T1[i]), K(T2[i])], writes=[K(B3, (b0, 1))])
            P.op('pool', lambda: nc.gpsimd.tensor_tensor(out=bs[:, :, 256:384], in0=t2[:, :, 0, :], in1=t1[:, :, 1, :], op=ALU.add),
                 reads=[K(T1[i]), K(T2[i])], writes=[K(B3, (b0, 2))])
            P.op('dve', lambda: nc.vector.scalar_tensor_tensor(out=bs[:, :, 0:128], in0=t2[:, :, 0, :], scalar=-1.0, in1=t1[:, :, 1, :],
                                                               op0=ALU.mult, op1=ALU.subtract),
                 reads=[K(T1[i]), K(T2[i])], writes=[K(B3, (b0, 0))])
            X = nextpb()
            for s in range(2):
                P.op('pe', lambda: nc.tensor.matmul(X[:, s * 256:(s + 1) * 256], lhsT=Fr_l, rhs=B3[:, b0 + s, 128:384], start=True, stop=False),
                     reads=[K(B3), K(cmb)], writes=[K(X)])
                P.op('pe', lambda: nc.tensor.matmul(X[:, s * 256:(s + 1) * 256], lhsT=Fi_l, rhs=B3[:, b0 + s, 0:256], start=False, stop=True),
                     reads=[K(B3), K(cmb)], writes=[K(X)])
            evac(s0, X)

    for half in range(2):
        P.dma('act', w3s[:], w3_d[half], writes=[K(w3s)])
        P.dma('act', cws[:], cw_d[half], writes=[K(cws)])
        cwv = cws[:, 0:9 * HC].rearrange("p (w j c) -> p w j c", w=3, j=3)
        cbv = cws[:, 9 * HC:12 * HC].rearrange("p (w c) -> p w c", w=3)
        skv = cws[:, 12 * HC:14 * HC].rearrange("p (o c) -> p o c", o=2)
        for fset in range(2 if nsets_ctx else 1):
            Ev = big[:, 0:128 * 2 * HC].rearrange("p (f d c) -> p f d c", f=128, d=2)
            P.dma('sp', big[:, 0:128 * 2 * HC], E_d[fset, half], writes=[K(big)])
            k2v = k2[:].rearrange("p (o c) f -> p o c f", o=2)
            for ch in range(32):
                zt = zts[ch % 2]
                P.dma('sp' if ch % 2 == 0 else 'act', zt[:], zT_d[fset, :, ch * 512:(ch + 1) * 512], writes=[K(zt)])
                p1 = nextpb()
                P.op('pe', lambda: nc.tensor.matmul(p1[0:64, :], lhsT=w1s[:, :], rhs=zt[:, :], start=True, stop=True),
                     reads=[K(w1s), K(zt)], writes=[K(p1)])
                sin_layer(p1[0:64, :], h1, 0)
                p2 = nextpb()
                P.op('pe', lambda: nc.tensor.matmul(p2[0:64, :], lhsT=w2s[:, :], rhs=h1[:, :], start=True, stop=True),
                     reads=[K(w2s), K(h1)], writes=[K(p2)])
                sin_layer(p2[0:64, :], h2, 1)
                p3 = nextpb()
                for ff in range(4):
                    P.op('pe', lambda: nc.tensor.matmul(p3[:, ff * 64:(ff + 1) * 64], lhsT=h2[:, ff * 128:(ff + 1) * 128], rhs=w3s[:, :],
                                                        start=True, stop=True), reads=[K(h2), K(w3s)], writes=[K(p3)])
                f0 = ch * 4
                p3v = p3[:, 0:256].rearrange("p (f d o c) -> p f d o c", f=4, d=2, o=2)
                tv = tmpE[:].rearrange("p (f d o c) -> p f d o c", f=4, d=2, o=2)
                for d_ in range(2):
                    P.op('dve', lambda: nc.vector.tensor_tensor(out=tv[:, :, d_], in0=p3v[:, :, d_],
                                                                in1=_bc(Ev[:, f0:f0 + 4, d_, :].unsqueeze(2), [128, 4, 2, HC]), op=ALU.mult),
                         reads=[K(p3), K(big)], writes=[K(tmpE, d_)])
                P.op('pool', lambda: nc.gpsimd.tensor_tensor(out=k2v[:, :, :, f0:f0 + 4].rearrange("p o c f -> p f o c"),
                                                             in0=tv[:, :, 0], in1=tv[:, :, 1], op=ALU.add),
                     reads=[K(tmpE)], writes=[K(k2, ch)])
            sqv = big[:, 0:2 * HC * 128].rearrange("p (s f) -> p s f", f=128)
            P.op('dve', lambda: nc.vector.tensor_tensor(out=sqv, in0=k2[:], in1=k2[:], op=ALU.mult), reads=[K(k2)], writes=[K(big)])
            P.op('dve', lambda: nc.vector.tensor_reduce(out=ssum[:, :], in_=sqv, axis=AX.X, op=ALU.add), reads=[K(big)], writes=[K(ssum)])
            pt = nextpb()
            P.op('pe', lambda: nc.tensor.matmul(pt[:, 0:2 * HC], lhsT=ones[:, :], rhs=ssum[:, :], start=True, stop=True),
                 reads=[K(ones), K(ssum)], writes=[K(pt)])
            P.op('dve', lambda: nc.vector.tensor_scalar(out=rn[:, :], in0=pt[:, 0:2 * HC], scalar1=EPS, scalar2=None, op0=ALU.add),
                 reads=[K(pt)], writes=[K(rn)])
            P.op('act', lambda: nc.scalar.activation(out=rn[:, :], in_=rn[:, :], func=AF.Sqrt), reads=[K(rn)], writes=[K(rn)])
            P.op('dve', lambda: nc.vector.reciprocal(out=rn[:, :], in_=rn[:, :]), reads=[K(rn)], writes=[K(rn)])
            P.op('dve', lambda: nc.vector.tensor_tensor(out=k2b[:], in0=k2[:], in1=_bc(rn[:, :].unsqueeze(2), [128, 2 * HC, 128]), op=ALU.mult),
                 reads=[K(k2), K(rn)], writes=[K(k2b)])

            def evac_H(s0, X):
                P.op('act', lambda: nc.scalar.mul(out=Hs[:, s0:s0 + 2, :], in_=X[:].rearrange("p (s k) -> p s k", s=2), mul=1.0 / NFFT),
                     reads=[K(X)], writes=[K(Hs, s0)])
            fwd_fft(k2b, 128, 2 * HC, evac_H)

            n1u = 64 if fset == 0 else 2
            for bb in range(2):
                st = fset * 2 + bb
                for wch in range(3):
                    Uv = big[0:64, 0:3 * HC * 128].rearrange("p (j c f) -> p j c f", j=3, c=HC)
                    P.dma('sp', big[0:64, 0:3 * HC * 128], U_d[st, half, wch], writes=[K(big)])
                    z_ = zs[wch]
                    e1, e2 = ('dve', 'pool') if wch % 2 == 0 else ('pool', 'dve')
                    E1, E2 = P.eng[e1], P.eng[e2]

                    def wj(j):
                        return _bc(cwv[:, wch, j, :].unsqueeze(2), [64, HC, 128])
                    P.op(e1, lambda: E1.tensor_tensor(out=ct1[:], in0=Uv[:, 0], in1=wj(0), op=ALU.mult), reads=[K(big), K(cws)], writes=[K(ct1)])
                    P.op(e2, lambda: E2.tensor_tensor(out=ct2[:], in0=Uv[:, 1], in1=wj(1), op=ALU.mult), reads=[K(big), K(cws)], writes=[K(ct2)])
                    P.op(e1, lambda: E1.tensor_tensor(out=ct1[:], in0=ct1[:], in1=ct2[:], op=ALU.add), reads=[K(ct1), K(ct2)], writes=[K(ct1)])
                    P.op(e2, lambda: E2.tensor_tensor(out=ct2[:], in0=Uv[:, 2], in1=wj(2), op=ALU.mult), reads=[K(big), K(cws)], writes=[K(ct2)])
                    P.op(e1, lambda: E1.tensor_tensor(out=ct1[:], in0=ct1[:], in1=ct2[:], op=ALU.add), reads=[K(ct1), K(ct2)], writes=[K(ct1)])
                    P.op(e2, lambda: E2.tensor_tensor(out=z_[:], in0=ct1[:], in1=_bc(cbv[:, wch, :].unsqueeze(2), [64, HC, 128]), op=ALU.add),
                         reads=[K(ct1), K(cws)], writes=[K(z_)])
                cur = zs[2]
                for o in range(2):
                    gate = zs[o]
                    P.op('act', lambda: nc.scalar.copy(out=zb[:], in_=cur[:]), reads=[K(cur)], writes=[K(zb)])

                    def evac_Y(s0, X):
                        i = (s0 // 2) % 2
                        X4 = X[:].rearrange("p (s r k) -> p s r k", s=2, r=2)
                        t1 = T1[i][:].rearrange("p s (r k) -> p s r k", r=2)
                        t2 = T2[i][:].rearrange("p s (r k) -> p s r k", r=2)
                        hr = _bc(Hs[:, o * HC + s0:o * HC + s0 + 2, 0:128].unsqueeze(2), [128, 2, 2, 128])
                        hi = _bc(Hs[:, o * HC + s0:o * HC + s0 + 2, 128:256].unsqueeze(2), [128, 2, 2, 128])
                        P.op('dve', lambda: nc.vector.tensor_tensor(out=t1, in0=X4, in1=hr, op=ALU.mult), reads=[K(X), K(Hs)], writes=[K(T1[i])])
                        P.op('dve', lambda: nc.vector.tensor_tensor(out=t2, in0=X4, in1=hi, op=ALU.mult), reads=[K(X), K(Hs)], writes=[K(T2[i])])
                        ys = Yb[:, s0:s0 + 2, :]
                        P.op('pool', lambda: nc.gpsimd.tensor_tensor(out=ys[:, :, 0:128], in0=t1[:, :, 0, :], in1=t2[:, :, 1, :], op=ALU.subtract),
                             reads=[K(T1[i]), K(T2[i])], writes=[K(Yb, (s0, 0))])
                        P.op('pool', lambda: nc.gpsimd.tensor_tensor(out=ys[:, :, 128:256], in0=t2[:, :, 0, :], in1=t1[:, :, 1, :], op=ALU.add),
                             reads=[K(T1[i]), K(T2[i])], writes=[K(Yb, (s0, 1))])
                        C = nextpb()
                        for s in range(2):
                            P.op('pe', lambda: nc.tensor.matmul(C[:, s * 256:(s + 1) * 256], lhsT=Yb[:, s0 + s, 0:128], rhs=rhsB, start=True, stop=False),
                                 reads=[K(Yb), K(cmb)], writes=[K(C)])
                            P.op('pe', lambda: nc.tensor.matmul(C[:, s * 256:(s + 1) * 256], lhsT=Yb[:, s0 + s, 128:256], rhs=rhsC, start=False, stop=True),
                                 reads=[K(Yb), K(cmb)], writes=[K(C)])
                        C4 = C[:].rearrange("p (s r k) -> p s r k", s=2, r=2)
                        u1 = T1[1 - i][:].rearrange("p s (r k) -> p s r k", r=2)
                        u2 = T2[1 - i][:].rearrange("p s (r k) -> p s r k", r=2)
                        P.op('dve', lambda: nc.vector.tensor_tensor(out=u1, in0=C4, in1=_bc(twr.unsqueeze(1).unsqueeze(1), [128, 2, 2, 128]), op=ALU.mult),
                             reads=[K(C), K(tws)], writes=[K(T1[1 - i])])
                        P.op('dve', lambda: nc.vector.tensor_tensor(out=u2, in0=C4, in1=_bc(twi.unsqueeze(1).unsqueeze(1), [128, 2, 2, 128]), op=ALU.mult),
                             reads=[K(C), K(tws)], writes=[K(T2[1 - i])])
                        P.op('pool', lambda: nc.gpsimd.tensor_tensor(out=Dr[:, s0:s0 + 2, :], in0=u1[:, :, 0, :], in1=u2[:, :, 1, :], op=ALU.add),
                             reads=[K(T1[1 - i]), K(T2[1 - i])], writes=[K(Dr, s0)])
                        P.op('pool', lambda: nc.gpsimd.tensor_tensor(out=Di[:, s0:s0 + 2, :], in0=u1[:, :, 1, :], in1=u2[:, :, 0, :], op=ALU.subtract),
                             reads=[K(T1[1 - i]), K(T2[1 - i])], writes=[K(Di, s0)])
                    fwd_fft(zb, n1u, HC, evac_Y)
                    for s0 in range(0, HC, 4):
                        Yp = nextpb()
                        P.op('pe', lambda: nc.tensor.matmul(Yp[0:64, :], lhsT=cmb[:, 0, 0:64], rhs=Dr[:, s0:s0 + 4, :], start=True, stop=False),
                             reads=[K(Dr), K(cmb)], writes=[K(Yp)])
                        P.op('pe', lambda: nc.tensor.matmul(Yp[0:64, :], lhsT=cmb[:, 1, 0:64], rhs=Di[:, s0:s0 + 4, :], start=False, stop=True),
                             reads=[K(Di), K(cmb)], writes=[K(Yp)])
                        c4 = ct1[:, s0:s0 + 4, :]
                        P.op('pool', lambda: nc.gpsimd.tensor_tensor(out=c4, in0=cur[:, s0:s0 + 4, :],
                                                                     in1=_bc(skv[:, o, s0:s0 + 4].unsqueeze(2), [64, 4, 128]), op=ALU.mult),
                             reads=[K(cur), K(cws)], writes=[K(ct1, s0)])
                        P.op('dve', lambda: nc.vector.tensor_tensor(out=c4, in0=c4, in1=Yp[0:64, :].rearrange("p (s f) -> p s f", s=4), op=ALU.add),
                             reads=[K(ct1, s0), K(Yp)], writes=[K(ct1, s0)])
                        P.op('pool', lambda: nc.gpsimd.tensor_tensor(out=zc[:, s0:s0 + 4, :], in0=c4, in1=gate[:, s0:s0 + 4, :], op=ALU.mult),
                             reads=[K(ct1, s0), K(gate)], writes=[K(zc, (o, s0))])
                    if o == 0:
                        P.op('act', lambda: nc.scalar.copy(out=zs[2][:], in_=zc[:]), reads=[K(zc)], writes=[K(zs[2])])
                P.dma('act', o_d[st, half], zc[:].rearrange("p c f -> p (c f)"), reads=[K(zc)], writes=[K(o_d)])
    P.finish([(P.dsem[o_d.name][2], P.dsem[o_d.name][1], None)])
    P.close()
    return nc


def run_l2b(hy_lat, hy_ctx, p, with_ctx):
    HC = 16
    nc = build_l2b(with_ctx)
    cm, tw = hyena_consts()
    w3 = p['hy_w3']
    in_maps = []
    for j in range(NCORES):
        zTs, Es, w3s, cws = [], [], [], []
        U = np.zeros((4, 2, 3, 64, 3, HC, 128), np.float32)
        for half in range(2):
            chans = np.arange(32 * j + HC * half, 32 * j + HC * (half + 1))
            sel = np.zeros((64, 2, 2, HC), np.float32)
            for d_ in range(2):
                for o in range(2):
                    sel[:, d_, o, :] = w3[:, o * 512 + d_ * 256 + chans]
            w3s.append(sel.reshape(64, -1))
            cwp = np.concatenate([
                np.stack([[p['hy_conv_w'][jj, w * 256 + chans] for jj in range(3)] for w in range(3)]).reshape(-1),
                np.stack([p['hy_conv_b'][w * 256 + chans] for w in range(3)]).reshape(-1),
                np.stack([p['hy_skip'][o, chans] for o in range(2)]).reshape(-1)]).astype(np.float32)
            cws.append(rep128(cwp)[:64])
            for st in range(4):
                src = hy_lat[st] if st < 2 else hy_ctx[st - 2]
                L = src.shape[0]
                for w in range(3):
                    u = src[:, w * 256 + chans]
                    up = np.concatenate([np.zeros((1, HC), np.float32), u, np.zeros((1, HC), np.float32)], axis=0)
                    for jj in range(3):
                        sh = up[jj:jj + L]
                        U[st, half, w, :L // 128, jj] = sh.reshape(L // 128, 128, HC).transpose(0, 2, 1)
        for fs, L in enumerate([N, CTX]):
            Eh = []
            for half in range(2):
                chans = np.arange(32 * j + HC * half, 32 * j + HC * (half + 1))
                zT, E = hyena_tap_tables(L, chans)
                Eh.append(E.reshape(128, -1))
            zTs.append(zT)
            Es.append(np.stack(Eh))
        in_maps.append({
            "cm": cm, "tw": tw, "zT": np.stack(zTs), "E": np.stack(Es),
            "w1": np.ascontiguousarray(p['hy_w1']), "w2": np.ascontiguousarray(p['hy_w2']), "w3": np.stack(w3s),
            "fb": np.ascontiguousarray(np.stack([p['hy_freq'], p['hy_b1'], p['hy_b2'], np.zeros(64, np.float32)], axis=1)),
            "U": U.reshape(4, 2, 3, 64, -1), "cw": np.stack(cws),
        })
    res = _run(nc, in_maps)
    out_l = np.zeros((B, N, 256), np.float32)
    out_c = np.zeros((B, CTX, 256), np.float32)
    for j in range(NCORES):
        o = res[j]["o_hy"].reshape(4, 2, 64, HC, 128)
        for half in range(2):
            cs = slice(32 * j + HC * half, 32 * j + HC * (half + 1))
            for b in range(2):
                out_l[b, :, cs] = o[b, half].transpose(0, 2, 1).reshape(N, HC)
                out_c[b, :, cs] = o[2 + b, half, :2].transpose(0, 2, 1).reshape(CTX, HC)
    return out_l, out_c


def pool_band(tile_start, L):
    out = np.zeros((3, 4, 128, 128), np.float32)
    t = tile_start + np.arange(128)
    for g, w in enumerate((2, 4, 8, 16)):
        lo = np.clip(t - w // 2, 0, L)
        hi = np.clip(t - w // 2 + w, 0, L)
        cnt = (hi - lo).astype(np.float32)
        for j in range(3):
            s = tile_start + (j - 1) * 128 + np.arange(128)
            inside = (s[:, None] >= lo[None, :]) & (s[:, None] < hi[None, :])
            out[j, g] = inside / cnt[None, :] - (s[:, None] == t[None, :])
    return out


def build_l3():
    nc = _new_nc()
    xw = nc.dram_tensor("xw", [ROWS, D], F32, kind="ExternalInput")
    mix = nc.dram_tensor("mix", [ROWS, 768], F32, kind="ExternalInput")
    pvw = nc.dram_tensor("pvw", [21 * 128, 256], F32, kind="ExternalInput")
    band = nc.dram_tensor("band", [4, 3, 4, 128, 128], F32, kind="ExternalInput")
    psc = nc.dram_tensor("psc", [128, 256], F32, kind="ExternalInput")
    wout = nc.dram_tensor("wout", [D, D], F32, kind="ExternalInput")
    mg1 = nc.dram_tensor("mg1", [2, 128, D], F32, kind="ExternalInput")
    msc = nc.dram_tensor("msc", [2, 128, D], F32, kind="ExternalInput")
    msh = nc.dram_tensor("msh", [2, 128, D], F32, kind="ExternalInput")
    n2w = nc.dram_tensor("n2w", [128, D], F32, kind="ExternalInput")
    wr = nc.dram_tensor("wr", [D, NE], F32, kind="ExternalInput")
    idn = nc.dram_tensor("idn", [128, 128], F32, kind="ExternalInput")
    o_xm = nc.dram_tensor("o_xm", [ROWS, D], F32, kind="ExternalOutput")
    o_h2 = nc.dram_tensor("o_h2", [ROWS, D], F32, kind="ExternalOutput")
    o_af = nc.dram_tensor("o_af", [ROWS, NE], F32, kind="ExternalOutput")
    P = Prog(nc)
    bands = P.sb("bands", [128, 4, 3, 4, 128], F32)
    pvs = P.sb("pvs", [128, 21, 256], F32)
    pscs = P.sb("pscs", [128, 256], F32)
    wob = P.sb("wob", [128, 8, D], BF16)
    wst = [P.sb("wst%d" % i, [128, D], F32) for i in range(2)]
    G1 = P.sb("G1", [128, 2, D], F32)
    Ak = P.sb("Ak", [128, 2, D], F32)
    Bk = P.sb("Bk", [128, 2, D], F32)
    nw = P.sb("nw", [128, D], F32)
    wrs = P.sb("wrs", [128, 8, NE], F32)
    ids = P.sb("ids", [128, 128], F32)
    idb = P.sb("idb", [128, 128], BF16)
    xt = [P.sb("xt%d" % i, [128, D], F32) for i in range(2)]
    mx = [P.sb("mx%d" % i, [128, D], F32) for i in range(2)]
    mb = P.sb("mb", [128, D], BF16)
    mT = P.sb("mT", [128, 8, 128], BF16)
    xm = P.sb("xm", [128, D], F32)
    scr = P.sb("scr", [128, D], BF16)
    ss = P.sb("ss", [128, 1], F32)
    rstd = P.sb("rstd", [128, 1], F32)
    h2 = P.sb("h2", [128, D], F32)
    h2T = P.sb("h2T", [128, 8, 128], F32)
    sm = P.sb("sm", [128, 4], F32)
    lg = P.sb("lg", [128, NE], F32)
    af = P.sb("af", [128, NE], F32)
    pP = P.ps("pP", [128, 512], F32)
    pT = P.ps("pT", [128, 8, 128], BF16)
    pW = [P.ps("pW%d" % i, [128, 512], F32) for i in range(2)]
    pT32 = [P.ps("pT32_%d" % i, [128, 4, 128], F32) for i in range(2)]
    pR = P.ps("pR", [128, 512], F32)

    P.dma('sp', bands[:], band.ap().rearrange("v j g s t -> s v j g t"), writes=[K(bands)])
    P.dma('act', pvs[:], pvw.ap().rearrange("(i p) c -> p i c", p=128), writes=[K(pvs)])
    P.dma('act', pscs[:], psc[:, :], writes=[K(pscs)])
    P.dma('pool', G1[:], mg1.ap().rearrange("k p d -> p k d"), writes=[K(G1)])
    P.dma('pool', Ak[:], msc.ap().rearrange("k p d -> p k d"), writes=[K(Ak)])
    P.dma('pool', Bk[:], msh.ap().rearrange("k p d -> p k d"), writes=[K(Bk)])
    P.dma('sp', nw[:], n2w[:, :], writes=[K(nw)])
    P.dma('sp', wrs[:], wr.ap().rearrange("(c p) e -> p c e", p=128), writes=[K(wrs)])
    P.dma('sp', ids[:], idn[:, :], writes=[K(ids)])
    P.op('act', lambda: nc.scalar.copy(out=idb[:], in_=ids[:]), reads=[K(ids)], writes=[K(idb)])
    for k in range(2):
        P.op('dve', lambda: nc.vector.scalar_tensor_tensor(out=Ak[:, k, :], in0=Ak[:, k, :], scalar=1.0, in1=nw[:],
                                                           op0=ALU.add, op1=ALU.mult), reads=[K(Ak), K(nw)], writes=[K(Ak)])
    for c in range(8):
        st = wst[c % 2]
        P.dma('sp' if c % 2 == 0 else 'act', st[:], wout[c * 128:(c + 1) * 128, :], writes=[K(st)])
        if c % 2 == 0:
            P.op('dve', lambda: nc.vector.tensor_copy(out=wob[:, c, :], in_=st[:]), reads=[K(st)], writes=[K(wob, c)])
        else:
            P.op('pool', lambda: nc.gpsimd.tensor_copy(out=wob[:, c, :], in_=st[:]), reads=[K(st)], writes=[K(wob, c)])
    outs = []
    for t in range(TPC):
        kind = 1 if t == TPC - 1 else 0
        slot = 0 if t == 0 else 2 if t == 15 else 3 if t == 16 else 1
        pbase = t if t < 16 else 18
        x_, m_ = xt[t % 2], mx[t % 2]
        P.dma('sp', x_[:], xw[t * 128:(t + 1) * 128, :], writes=[K(x_)])
        P.dma('act', m_[:, 0:768], mix[t * 128:(t + 1) * 128, :], writes=[K(m_, 0)])
        for g in range(4):
            for j in range(3):
                P.op('pe', lambda: nc.tensor.matmul(pP[:, g * 64:(g + 1) * 64], lhsT=bands[:, slot, j, g, :],
                                                    rhs=pvs[:, pbase + j, g * 64:(g + 1) * 64], start=(j == 0), stop=(j == 2)),
                     reads=[K(bands), K(pvs)], writes=[K(pP)])
        P.op('dve', lambda: nc.vector.tensor_tensor(out=m_[:, 768:1024], in0=pP[:, 0:256], in1=pscs[:], op=ALU.mult),
             reads=[K(pP), K(pscs)], writes=[K(m_, 1)])
        P.op('act', lambda: nc.scalar.copy(out=mb[:], in_=m_[:]), reads=[K(m_)], writes=[K(mb)])
        for c in range(8):
            P.op('pe', lambda: nc.tensor.transpose(out=pT[:, c, :], in_=mb[:, c * 128:(c + 1) * 128], identity=idb[:]),
                 reads=[K(mb), K(idb)], writes=[K(pT)])
        P.op('act', lambda: nc.scalar.copy(out=mT[:], in_=pT[:]), reads=[K(pT)], writes=[K(mT)])
        for hh in range(2):
            for c in range(8):
                P.op('pe', lambda: nc.tensor.matmul(pW[hh][:], lhsT=mT[:, c, :], rhs=wob[:, c, hh * 512:(hh + 1) * 512],
                                                    start=(c == 0), stop=(c == 7)), reads=[K(mT), K(wob)], writes=[K(pW[hh])])
            P.op('dve', lambda: nc.vector.tensor_tensor(out=xm[:, hh * 512:(hh + 1) * 512], in0=pW[hh][:],
                                                        in1=G1[:, kind, hh * 512:(hh + 1) * 512], op=ALU.mult),
                 reads=[K(pW[hh]), K(G1)], writes=[K(xm, hh)])
        P.op('pool', lambda: nc.gpsimd.tensor_tensor(out=xm[:], in0=xm[:], in1=x_[:], op=ALU.add), reads=[K(xm), K(x_)], writes=[K(xm)])
        outs.append(P.dma('sp', o_xm[t * 128:(t + 1) * 128, :], xm[:], reads=[K(xm)], writes=[K(o_xm, t)]))
        emit_norm_mod(P, nc, xm, K(xm), Ak, Bk, kind, None, scr, ss, rstd, t)
        P.op('dve', lambda: nc.vector.scalar_tensor_tensor(out=h2[:], in0=xm[:], scalar=rstd[:, 0:1], in1=Ak[:, kind, :],
                                                           op0=ALU.mult, op1=ALU.mult), reads=[K(xm), K(rstd), K(Ak)], writes=[K(h2)])
        P.op('pool', lambda: nc.gpsimd.tensor_tensor(out=h2[:], in0=h2[:], in1=Bk[:, kind, :], op=ALU.add),
             reads=[K(h2), K(Bk)], writes=[K(h2)])
        outs.append(P.dma('act', o_h2[t * 128:(t + 1) * 128, :], h2[:], reads=[K(h2)], writes=[K(o_h2, t)]))
        for c in range(8):
            pt = pT32[c // 4]
            P.op('pe', lambda: nc.tensor.transpose(out=pt[:, c % 4, :], in_=h2[:, c * 128:(c + 1) * 128], identity=ids[:]),
                 reads=[K(h2), K(ids)], writes=[K(pt)])
        P.op('act', lambda: nc.scalar.copy(out=h2T[:, 0:4, :], in_=pT32[0][:]), reads=[K(pT32[0])], writes=[K(h2T, 0)])
        P.op('dve', lambda: nc.vector.tensor_copy(out=h2T[:, 4:8, :], in_=pT32[1][:]), reads=[K(pT32[1])], writes=[K(h2T, 1)])
        for c in range(8):
            P.op('pe', lambda: nc.tensor.matmul(pR[:, 0:NE], lhsT=h2T[:, c, :], rhs=wrs[:, c, :], start=(c == 0), stop=(c == 7)),
                 reads=[K(h2T), K(wrs)], writes=[K(pR)])
        P.op('dve', lambda: nc.vector.tensor_copy(out=lg[:], in_=pR[:, 0:NE]), reads=[K(pR)], writes=[K(lg)])
        P.op('dve', lambda: nc.vector.tensor_reduce(out=sm[:, 0:1], in_=lg[:], axis=AX.X, op=ALU.max), reads=[K(lg)], writes=[K(sm)])
        P.op('dve', lambda: nc.vector.tensor_scalar(out=sm[:, 1:2], in0=sm[:, 0:1], scalar1=-1.0, scalar2=None, op0=ALU.mult),
             reads=[K(sm)], writes=[K(sm)])
        P.op('act', lambda: nc.scalar.activation(out=af[:], in_=lg[:], func=AF.Exp, bias=sm[:, 1:2], scale=1.0, accum_out=sm[:, 2:3]),
             reads=[K(lg), K(sm)], writes=[K(af), K(sm)])
        P.op('dve', lambda: nc.vector.reciprocal(out=sm[:, 3:4], in_=sm[:, 2:3]), reads=[K(sm)], writes=[K(sm)])
        P.op('act', lambda: nc.scalar.activation(out=af[:], in_=af[:], func=AF.Copy, scale=sm[:, 3:4]), reads=[K(af), K(sm)], writes=[K(af)])
        outs.append(P.dma('sp', o_af[t * 128:(t + 1) * 128, :], af[:], reads=[K(af)], writes=[K(o_af, t)]))
    P.finish(outs)
    P.close()
    return nc


def _padrows(a, lo, hi):
    out = np.zeros((hi - lo, a.shape[1]), a.dtype)
    l2, h2 = max(lo, 0), min(hi, a.shape[0])
    out[l2 - lo:h2 - lo] = a[l2:h2]
    return out


def run_l3(xl, xc, ona, oc, hyl, hyc, pvl, pvc, mod, p):
    nc = build_l3()
    in_maps = []
    for j in range(NCORES):
        b, t0, cb, ct0 = core_rows(j)
        q = j % 4
        xw = np.concatenate([xl[b, t0:t0 + 2048], xc[cb, ct0:ct0 + 128]], axis=0)
        mixl = np.concatenate([ona[b, t0:t0 + 2048], hyl[b, t0:t0 + 2048]], axis=1)
        mixc = np.concatenate([oc[cb, ct0:ct0 + 128], hyc[cb, ct0:ct0 + 128]], axis=1)
        pvw = np.concatenate([_padrows(pvl[b], t0 - 128, t0 + 2048 + 128), _padrows(pvc[cb], ct0 - 128, ct0 + 256)], axis=0)
        band = np.stack([pool_band(t0, N), pool_band(t0 + 128, N), pool_band(t0 + 15 * 128, N), pool_band(ct0, CTX)])
        in_maps.append({
            "xw": np.ascontiguousarray(xw), "mix": np.ascontiguousarray(np.concatenate([mixl, mixc], axis=0).astype(np.float32)),
            "pvw": np.ascontiguousarray(pvw), "band": band, "psc": rep128(p['pool_scale']),
            "wout": np.ascontiguousarray(p['w_out']),
            "mg1": np.stack([rep128(mod[b, 2 * D:3 * D]), rep128(mod[2, 2 * D:3 * D])]),
            "msc": np.stack([rep128(mod[b, 4 * D:5 * D]), rep128(mod[2, 4 * D:5 * D])]),
            "msh": np.stack([rep128(mod[b, 3 * D:4 * D]), rep128(mod[2, 3 * D:4 * D])]),
            "n2w": rep128(p['norm2_w']), "wr": np.ascontiguousarray(p['w_router']), "idn": np.eye(128, dtype=np.float32),
        })
    res = _run(nc, in_maps)
    out = {}
    for name in ["o_xm", "o_h2", "o_af"]:
        full = np.stack([res[j][name] for j in range(NCORES)])
        lat = full[:, :2048].reshape(B, N, -1)
        cx = np.stack([np.concatenate([full[2 * cb + h, 2048:] for h in range(2)], axis=0) for cb in range(B)])
        out[name] = (lat, cx)
    return out


def build_l4():
    nc = _new_nc()
    aT = nc.dram_tensor("aT", [64, N], F32, kind="ExternalInput")
    capd = nc.dram_tensor("cap", [64, 1], F32, kind="ExternalInput")
    o_m = nc.dram_tensor("o_m", [64, N], F32, kind="ExternalOutput")
    P = Prog(nc)
    a = P.sb("a", [64, N], F32)
    cmp_ = P.sb("cmp", [64, N], F32)
    cap = P.sb("capS", [64, 1], F32)
    st = P.sb("st", [64, 4], F32)
    P.dma('sp', a[:], aT[:, :], writes=[K(a)])
    P.dma('act', cap[:], capd[:, :], writes=[K(cap)])
    P.op('dve', lambda: nc.vector.memset(st[:], 0.0), writes=[K(st)])
    for it in range(26):
        w = 2.0 ** -(it + 1)
        P.op('dve', lambda: nc.vector.tensor_scalar(out=st[:, 1:2], in0=st[:, 0:1], scalar1=w, scalar2=None, op0=ALU.add),
             reads=[K(st)], writes=[K(st)])
        P.op('dve', lambda: nc.vector.tensor_tensor(out=cmp_[:], in0=a[:], in1=_bc(st[:, 1:2], [64, N]), op=ALU.is_ge),
             reads=[K(a), K(st)], writes=[K(cmp_)])
        P.op('dve', lambda: nc.vector.tensor_reduce(out=st[:, 2:3], in_=cmp_[:], axis=AX.X, op=ALU.add), reads=[K(cmp_)], writes=[K(st)])
        P.op('dve', lambda: nc.vector.tensor_tensor(out=st[:, 3:4], in0=st[:, 2:3], in1=cap[:, 0:1], op=ALU.is_ge),
             reads=[K(st), K(cap)], writes=[K(st)])
        P.op('dve', lambda: nc.vector.scalar_tensor_tensor(out=st[:, 0:1], in0=st[:, 3:4], scalar=w, in1=st[:, 0:1],
                                                           op0=ALU.mult, op1=ALU.add), reads=[K(st)], writes=[K(st)])
    P.op('dve', lambda: nc.vector.tensor_tensor(out=cmp_[:], in0=a[:], in1=_bc(st[:, 0:1], [64, N]), op=ALU.is_ge),
         reads=[K(a), K(st)], writes=[K(cmp_)])
    tok = P.dma('sp', o_m[:, :], cmp_[:], reads=[K(cmp_)], writes=[K(o_m)])
    P.finish([tok])
    P.close()
    return nc


def run_l4(aff_l, aff_c):
    nc = build_l4()
    aT = np.zeros((64, N), np.float32)
    cap = np.zeros((64, 1), np.float32)
    for b in range(B):
        aT[b * 16:(b + 1) * 16] = aff_l[b].T
        aT[32 + b * 16:32 + (b + 1) * 16, :CTX] = aff_c[b].T
    cap[:32] = 2 * N // NE
    cap[32:] = 2 * CTX // NE
    res = _run(nc, [{"aT": aT, "cap": cap} for _ in range(NCORES)])
    m = res[0]["o_m"] > 0.5
    return m[:32].reshape(B, NE, N), m[32:, :CTX].reshape(B, NE, CTX)


ER = 2176


def build_l5():
    nc = _new_nc()
    xg = nc.dram_tensor("xg", [2, D, ER], F32, kind="ExternalInput")
    gt = nc.dram_tensor("gt", [2, 128, 17], F32, kind="ExternalInput")
    wg = nc.dram_tensor("wg", [2, D, DE], F32, kind="ExternalInput")
    wu = nc.dram_tensor("wu", [2, D, DE], F32, kind="ExternalInput")
    wd = nc.dram_tensor("wd", [2, DE, D], F32, kind="ExternalInput")
    o_y = nc.dram_tensor("o_y", [2, ER, D], F32, kind="ExternalOutput")
    P = Prog(nc)
    wgb = P.sb("wgb", [128, 8, DE], BF16)
    wub = P.sb("wub", [128, 8, DE], BF16)
    wdb = P.sb("wdb", [128, 16, D], BF16)
    xb = P.sb("xb", [128, 8, ER], BF16)
    stg = [P.sb("stg%d" % i, [128, ER], F32) for i in range(2)]
    gts = P.sb("gts", [128, 2, 17], F32)
    hT = P.sb("hT", [128, 16, 512], BF16)
    sa = [P.sb("sa%d" % i, [128, 512], F32) for i in range(2)]
    ysb = [P.sb("ysb%d" % i, [128, D], F32) for i in range(2)]
    pa = [P.ps("pa%d" % i, [128, 512], F32) for i in range(2)]
    pu = [P.ps("pu%d" % i, [128, 512], F32) for i in range(2)]
    py = [P.ps("py%d" % i, [128, 512], F32) for i in range(4)]
    P.dma('sp', gts[:], gt.ap().rearrange("e p t -> p e t"), writes=[K(gts)])
    outs = []
    ci = [0]

    def load_cast(dst_ap, kdst, src_ap, ncol):
        i = ci[0] % 2
        ci[0] += 1
        s_ = stg[i]
        P.dma('sp' if i == 0 else 'act', s_[:, 0:ncol], src_ap, writes=[K(s_)])
        if i == 0:
            P.op('dve', lambda: nc.vector.tensor_copy(out=dst_ap, in_=s_[:, 0:ncol]), reads=[K(s_)], writes=[kdst])
        else:
            P.op('pool', lambda: nc.gpsimd.tensor_copy(out=dst_ap, in_=s_[:, 0:ncol]), reads=[K(s_)], writes=[kdst])

    yi = [0]
    for e in range(2):
        for c in range(8):
            load_cast(wgb[:, c, :], K(wgb, c), wg[e, c * 128:(c + 1) * 128, :], DE)
            load_cast(wub[:, c, :], K(wub, c), wu[e, c * 128:(c + 1) * 128, :], DE)
            load_cast(xb[:, c, :], K(xb, c), xg[e, c * 128:(c + 1) * 128, :], ER)
        for f in range(16):
            load_cast(wdb[:, f, :], K(wdb, f), wd[e, f * 128:(f + 1) * 128, :], D)
        for rg in range(5):
            r0 = rg * 512
            n = 512 if rg < 4 else 128
            for f in range(16):
                i = f % 2
                for c in range(8):
                    P.op('pe', lambda: nc.tensor.matmul(pa[i][:, 0:n], lhsT=wgb[:, c, f * 128:(f + 1) * 128], rhs=xb[:, c, r0:r0 + n],
                                                        start=(c == 0), stop=(c == 7)), reads=[K(wgb), K(xb)], writes=[K(pa[i])])
                for c in range(8):
                    P.op('pe', lambda: nc.tensor.matmul(pu[i][:, 0:n], lhsT=wub[:, c, f * 128:(f + 1) * 128], rhs=xb[:, c, r0:r0 + n],
                                                        start=(c == 0), stop=(c == 7)), reads=[K(wub), K(xb)], writes=[K(pu[i])])
                P.op('act', lambda: nc.scalar.activation(out=sa[i][:, 0:n], in_=pa[i][:, 0:n], func=AF.Silu), reads=[K(pa[i])], writes=[K(sa[i])])
                P.op('dve', lambda: nc.vector.tensor_tensor(out=hT[:, f, 0:n], in0=sa[i][:, 0:n], in1=pu[i][:, 0:n], op=ALU.mult),
                     reads=[K(sa[i]), K(pu[i])], writes=[K(hT, f)])
            for tt in range(n // 128):
                tile_i = rg * 4 + tt
                y_ = ysb[yi[0] % 2]
                for hh in range(2):
                    pyt = py[(yi[0] % 2) * 2 + hh]
                    for f in range(16):
                        P.op('pe', lambda: nc.tensor.matmul(pyt[:], lhsT=hT[:, f, tt * 128:(tt + 1) * 128], rhs=wdb[:, f, hh * 512:(hh + 1) * 512],
                                                            start=(f == 0), stop=(f == 15)), reads=[K(hT), K(wdb)], writes=[K(pyt)])
                    P.op('act', lambda: nc.scalar.activation(out=y_[:, hh * 512:(hh + 1) * 512], in_=pyt[:], func=AF.Copy,
                                                             scale=gts[:, e, tile_i:tile_i + 1]), reads=[K(pyt), K(gts)], writes=[K(y_, hh)])
                outs.append(P.dma('sp' if yi[0] % 2 == 0 else 'act', o_y[e, tile_i * 128:(tile_i + 1) * 128, :], y_[:],
                                  reads=[K(y_)], writes=[K(o_y, (e, tile_i))]))
                yi[0] += 1
    P.finish(outs)
    P.close()
    return nc


def run_l5(h2l, h2c, aff_l, aff_c, ml, mc, p):
    nc = build_l5()
    in_maps, meta = [], []
    for j in range(NCORES):
        xg = np.zeros((2, ER, D), np.float32)
        gt = np.zeros((2, ER), np.float32)
        ids = np.full((2, ER), -1, np.int64)
        for k in range(2):
            e = 2 * j + k
            r = 0
            for b in range(B):
                idx = np.nonzero(ml[b, e])[0][:1024]
                xg[k, r:r + len(idx)] = h2l[b, idx]
                gt[k, r:r + len(idx)] = aff_l[b, idx, e]
                ids[k, r:r + len(idx)] = b * N + idx
                r += 1024
            for b in range(B):
                idx = np.nonzero(mc[b, e])[0][:32]
                xg[k, r:r + len(idx)] = h2c[b, idx]
                gt[k, r:r + len(idx)] = aff_c[b, idx, e]
                ids[k, r:r + len(idx)] = B * N + b * CTX + idx
                r += 32
        meta.append(ids)
        in_maps.append({
            "xg": np.ascontiguousarray(xg.transpose(0, 2, 1)),
            "gt": np.ascontiguousarray(gt.reshape(2, 17, 128).transpose(0, 2, 1)),
            "wg": np.ascontiguousarray(p['w_gate'][2 * j:2 * j + 2]), "wu": np.ascontiguousarray(p['w_up'][2 * j:2 * j + 2]),
            "wd": np.ascontiguousarray(p['w_down'][2 * j:2 * j + 2]),
        })
    res = _run(nc, in_maps)
    Y = np.concatenate([res[j]["o_y"].reshape(2 * ER, D) for j in range(NCORES)], axis=0)
    ids = np.concatenate([m.reshape(-1) for m in meta])
    return Y, ids


YC = 8192


def build_l6():
    nc = _new_nc()
    xm = nc.dram_tensor("xm", [ROWS, D], F32, kind="ExternalInput")
    yc = nc.dram_tensor("yc", [YC + 128, D], F32, kind="ExternalInput")
    ix = nc.dram_tensor("ix", [ROWS, NE], I32, kind="ExternalInput")
    mg2 = nc.dram_tensor("mg2", [2, 128, D], F32, kind="ExternalInput")
    o_x = nc.dram_tensor("o_x", [ROWS, D], F32, kind="ExternalOutput")
    P = Prog(nc)
    G2 = P.sb("G2", [128, 2, D], F32)
    ixs = P.sb("ixs", [128, TPC, NE], I32)
    xt = [P.sb("xt%d" % i, [128, D], F32) for i in range(2)]
    gb = [P.sb("gb%d" % i, [128, D], F32) for i in range(4)]
    acc = [P.sb("acc%d" % i, [128, D], F32) for i in range(2)]
    P.dma('sp', G2[:], mg2.ap().rearrange("k p d -> p k d"), writes=[K(G2)])
    P.dma('act', ixs[:], ix.ap().rearrange("(t p) e -> p t e", p=128), writes=[K(ixs)])
    outs = []
    gi = 0
    for t in range(TPC):
        kind = 1 if t == TPC - 1 else 0
        x_, a_ = xt[t % 2], acc[t % 2]
        P.dma('sp', x_[:], xm[t * 128:(t + 1) * 128, :], writes=[K(x_)])
        for k in range(NE):
            g_ = gb[gi % 4]
            gi += 1
            P.idma(out=g_[:, :], out_offset=None, in_=yc[:, :],
                   in_offset=bass.IndirectOffsetOnAxis(ap=ixs[:, t, k:k + 1], axis=0), reads=[K(ixs)], writes=[K(g_)])
            if k == 0:
                P.op('dve', lambda: nc.vector.tensor_copy(out=a_[:], in_=g_[:]), reads=[K(g_)], writes=[K(a_)])
            else:
                P.op('dve', lambda: nc.vector.tensor_tensor(out=a_[:], in0=a_[:], in1=g_[:], op=ALU.add), reads=[K(g_), K(a_)], writes=[K(a_)])
        P.op('dve', lambda: nc.vector.tensor_tensor(out=a_[:], in0=a_[:], in1=G2[:, kind, :], op=ALU.mult), reads=[K(a_), K(G2)], writes=[K(a_)])
        P.op('dve', lambda: nc.vector.tensor_tensor(out=a_[:], in0=a_[:], in1=x_[:], op=ALU.add), reads=[K(a_), K(x_)], writes=[K(a_)])
        outs.append(P.dma('act', o_x[t * 128:(t + 1) * 128, :], a_[:], reads=[K(a_)], writes=[K(o_x, t)]))
    P.finish(outs)
    P.close()
    return nc


def run_l6(xm_l, xm_c, Y, ids, mod):
    nc = build_l6()
    order = np.argsort(ids, kind='stable')
    sid = ids[order]
    in_maps = []
    for j in range(NCORES):
        b, t0, cb, ct0 = core_rows(j)
        gl = np.concatenate([b * N + t0 + np.arange(2048), B * N + cb * CTX + ct0 + np.arange(128)])
        ix = np.full((ROWS, NE), YC, np.int32)
        rows = []
        nxt = 0
        lo = np.searchsorted(sid, gl, side='left')
        hi = np.searchsorted(sid, gl, side='right')
        for r in range(ROWS):
            n = hi[r] - lo[r]
            if n:
                ix[r, :n] = nxt + np.arange(n)
                rows.append(order[lo[r]:hi[r]])
                nxt += n
        if nxt > YC:
            raise RuntimeError("expert hit capacity exceeded")
        yc = np.zeros((YC + 128, D), np.float32)
        if rows:
            yc[:nxt] = Y[np.concatenate(rows)]
        in_maps.append({
            "xm": np.ascontiguousarray(np.concatenate([xm_l[b, t0:t0 + 2048], xm_c[cb, ct0:ct0 + 128]], axis=0)),
            "yc": yc, "ix": ix, "mg2": np.stack([rep128(mod[b, 5 * D:6 * D]), rep128(mod[2, 5 * D:6 * D])]),
        })
    res = _run(nc, in_maps)
    full = np.stack([res[j]["o_x"] for j in range(NCORES)])
    lat = full[:, :2048].reshape(B, N, D)
    cx = np.stack([np.concatenate([full[2 * cb + h, 2048:] for h in range(2)], axis=0) for cb in range(B)])
    return lat, cx


PNAMES = ['w_mod', 'b_mod', 'norm1_w', 'norm2_w', 'w_in', 'w_out', 'q_norm_w', 'k_norm_w', 'na_rpb', 'hy_conv_w', 'hy_conv_b',
          'hy_w1', 'hy_b1', 'hy_w2', 'hy_b2', 'hy_w3', 'hy_freq', 'hy_skip', 'pool_w', 'pool_scale', 'w_router', 'w_gate',
          'w_up', 'w_down']


def kernel(**inputs):
    inp = {k: np.asarray(v) for k, v in inputs.items()}
    xl = np.ascontiguousarray(inp['x'], dtype=np.float32)
    xc = np.ascontiguousarray(inp['ctx'], dtype=np.float32)
    mod = run_l0(inp['c'], inp['c_ctx'], inp['w_mod'], inp['b_mod'])
    for lyr in range(2):
        p = {n: inp[n][lyr] for n in PNAMES}
        o1 = run_l1(xl, xc, mod[lyr], lyr, p)
        ona, oc = run_l2a(o1, p['na_rpb'], True)
        hyl, hyc = run_l2b(o1['o_hy'][0], o1['o_hy'][1], p, True)
        o3 = run_l3(xl, xc, ona, oc, hyl, hyc, o1['o_pv'][0], o1['o_pv'][1], mod[lyr], p)
        aff_l, aff_c = o3['o_af']
        ml, mc = run_l4(aff_l, aff_c)
        Y, ids = run_l5(o3['o_h2'][0], o3['o_h2'][1], aff_l, aff_c, ml, mc, p)
        xl, xc = run_l6(o3['o_xm'][0], o3['o_xm'][1], Y, ids, mod[lyr])
    return np.ascontiguousarray(xl.astype(np.float32))
```
